# Optimizing a Trainium2 kernel written in Bass

```python
import math
import jax, jax.numpy as jnp
from jax import lax
import numpy as np

D_MODEL = 2048
BATCH = 2
SEQ = 16384
DEPTH = 2

N_MIXERS = 2
DIFF_HEADS = 8
DIFF_HEAD_DIM = D_MODEL // DIFF_HEADS // 2
FOX_HEADS = 16
FOX_HEAD_DIM = D_MODEL // FOX_HEADS
D_FF = 5632
REL_BUCKETS = 32
REL_MAX_DIST = 128
Q_BLOCK = 128
ALPHA = (2 * DEPTH) ** 0.25
BETA = (8 * DEPTH) ** -0.25
LN_EPS = 1e-5
N_DIFF = (DEPTH + 1) // 2
N_FOX = DEPTH // 2

kernel_name = "hybrid_diff_fox_macaron_deepnorm"


def layer_norm(x, g, b):
    xf = x.astype(jnp.float32)
    mu = jnp.mean(xf, axis=-1, keepdims=True)
    var = jnp.mean(jnp.square(xf - mu), axis=-1, keepdims=True)
    return ((xf - mu) * lax.rsqrt(var + LN_EPS) * g + b).astype(x.dtype)


def rms_norm(x, g):
    xf = x.astype(jnp.float32)
    return (xf * lax.rsqrt(jnp.mean(jnp.square(xf), axis=-1, keepdims=True) + LN_EPS) * g).astype(x.dtype)


def swiglu(x, wg, wu, wd):
    return (jax.nn.silu(x @ wg) * (x @ wu)) @ wd


def t5_bucket(rel):
    n = jnp.maximum(rel, 0)
    max_exact = REL_BUCKETS // 2
    nf = jnp.maximum(n, 1).astype(jnp.float32)
    large = max_exact + (jnp.log(nf / max_exact) / math.log(REL_MAX_DIST / max_exact)
                         * (REL_BUCKETS - max_exact)).astype(jnp.int32)
    large = jnp.minimum(large, REL_BUCKETS - 1)
    return jnp.where(n < max_exact, n, large)


def diff_attention(x, w_qkv, w_o, lq1, lk1, lq2, lk2, sub_g, rel_table, lam_init):
    B, S, D = x.shape
    H, d = DIFF_HEADS, DIFF_HEAD_DIM
    nb = S // Q_BLOCK
    q, k, v = jnp.split(x @ w_qkv, 3, axis=-1)
    q = q.reshape(B, S, H, 2, d)
    k = k.reshape(B, S, H, 2, d)
    v = v.reshape(B, S, H, 2 * d)
    scale = d ** -0.5
    lam = (jnp.exp(jnp.sum(lq1.astype(jnp.float32) * lk1.astype(jnp.float32)))
           - jnp.exp(jnp.sum(lq2.astype(jnp.float32) * lk2.astype(jnp.float32))) + lam_init)
    qb = q.reshape(B, nb, Q_BLOCK, H, 2, d).transpose(1, 0, 2, 3, 4, 5)
    kpos = jnp.arange(S)

    def block(args):
        qi, bi = args
        qpos = bi * Q_BLOCK + jnp.arange(Q_BLOCK)
        rel = qpos[:, None] - kpos[None, :]
        s = jnp.einsum('bqhmd,bkhmd->bhmqk', qi, k,
                       preferred_element_type=jnp.float32) * scale
        bias = rel_table.astype(jnp.float32)[t5_bucket(rel)].transpose(2, 0, 1)
        s = jnp.where(rel >= 0, s + bias[None, :, None], -jnp.inf)
        p = jax.nn.softmax(s, axis=-1)
        a = p[:, :, 0] - lam * p[:, :, 1]
        return jnp.einsum('bhqk,bkhe->bqhe', a.astype(v.dtype), v)

    o = lax.map(block, (qb, jnp.arange(nb)))
    o = o.transpose(1, 0, 2, 3, 4).reshape(B, S, H, 2 * d)
    o = rms_norm(o, sub_g) * (1.0 - lam_init)
    return o.reshape(B, S, D) @ w_o


def forgetting_attention(x, w_qkv, w_o, w_f, b_f):
    B, S, D = x.shape
    H, d = FOX_HEADS, FOX_HEAD_DIM
    nb = S // Q_BLOCK
    q, k, v = jnp.split(x @ w_qkv, 3, axis=-1)
    q = q.reshape(B, S, H, d)
    k = k.reshape(B, S, H, d)
    v = v.reshape(B, S, H, d)
    logf = jax.nn.log_sigmoid((x @ w_f + b_f).astype(jnp.float32))
    c = jnp.cumsum(logf, axis=1).transpose(0, 2, 1)
    qb = q.reshape(B, nb, Q_BLOCK, H, d).transpose(1, 0, 2, 3, 4)
    cb = c.reshape(B, H, nb, Q_BLOCK).transpose(2, 0, 1, 3)
    kpos = jnp.arange(S)
    scale = d ** -0.5

    def block(args):
        qi, ci, bi = args
        qpos = bi * Q_BLOCK + jnp.arange(Q_BLOCK)
        causal = qpos[:, None] >= kpos[None, :]
        s = jnp.einsum('bqhd,bkhd->bhqk', qi, k, preferred_element_type=jnp.float32) * scale
        s = s + ci[..., None] - c[:, :, None, :]
        s = jnp.where(causal, s, -jnp.inf)
        p = jax.nn.softmax(s, axis=-1)
        return jnp.einsum('bhqk,bkhd->bqhd', p.astype(v.dtype), v)

    o = lax.map(block, (qb, cb, jnp.arange(nb)))
    o = o.transpose(1, 0, 2, 3, 4).reshape(B, S, D)
    return o @ w_o


def setup_inputs(seed: int = 0) -> dict:
    key = jax.random.key(seed)
    ks = jax.random.split(key, 24)
    D, F = D_MODEL, D_FF
    sd = D ** -0.5
    nrm = lambda k, shp, s: jax.random.normal(k, shp, jnp.float32) * s

    def qkv_weight(k, n, out_qk):
        kq, kv = jax.random.split(k)
        qk = nrm(kq, (n, D, 2 * out_qk), sd)
        vv = nrm(kv, (n, D, out_qk), sd * BETA)
        return jnp.concatenate([qk, vv], axis=-1)

    return {
        "x": jax.random.normal(ks[0], (BATCH, SEQ, D), jnp.float32),
        "rel_table": nrm(ks[1], (REL_BUCKETS, DIFF_HEADS), 0.5),
        "ffn1_wg": nrm(ks[2], (DEPTH, D, F), sd),
        "ffn1_wu": nrm(ks[3], (DEPTH, D, F), sd * BETA),
        "ffn1_wd": nrm(ks[4], (DEPTH, F, D), F ** -0.5 * BETA),
        "ffn2_wg": nrm(ks[5], (DEPTH, D, F), sd),
        "ffn2_wu": nrm(ks[6], (DEPTH, D, F), sd * BETA),
        "ffn2_wd": nrm(ks[7], (DEPTH, F, D), F ** -0.5 * BETA),
        "ln_g": 1.0 + nrm(ks[8], (DEPTH, 3, D), 0.02),
        "ln_b": nrm(ks[9], (DEPTH, 3, D), 0.02),
        "diff_wqkv": qkv_weight(ks[10], N_DIFF, D),
        "diff_wo": nrm(ks[11], (N_DIFF, D, D), sd * BETA),
        "diff_lq1": nrm(ks[12], (N_DIFF, DIFF_HEAD_DIM), 0.1),
        "diff_lk1": nrm(ks[13], (N_DIFF, DIFF_HEAD_DIM), 0.1),
        "diff_lq2": nrm(ks[14], (N_DIFF, DIFF_HEAD_DIM), 0.1),
        "diff_lk2": nrm(ks[15], (N_DIFF, DIFF_HEAD_DIM), 0.1),
        "diff_subln_g": 1.0 + nrm(ks[16], (N_DIFF, 2 * DIFF_HEAD_DIM), 0.02),
        "fox_wqkv": qkv_weight(ks[17], N_FOX, D),
        "fox_wo": nrm(ks[18], (N_FOX, D, D), sd * BETA),
        "fox_wf": nrm(ks[19], (N_FOX, D, FOX_HEADS), sd),
        "fox_bf": jax.random.uniform(ks[20], (N_FOX, FOX_HEADS), jnp.float32, 1.0, 6.0),
    }


def reference(x, rel_table, ffn1_wg, ffn1_wu, ffn1_wd, ffn2_wg, ffn2_wu, ffn2_wd, ln_g, ln_b,
              diff_wqkv, diff_wo, diff_lq1, diff_lk1, diff_lq2, diff_lk2, diff_subln_g,
              fox_wqkv, fox_wo, fox_wf, fox_bf):
    for i in range(DEPTH):
        j = i // N_MIXERS
        x = layer_norm(ALPHA * x + 0.5 * swiglu(x, ffn1_wg[i], ffn1_wu[i], ffn1_wd[i]), ln_g[i, 0], ln_b[i, 0])
        if i % N_MIXERS == 0:
            lam_init = 0.8 - 0.6 * math.exp(-0.3 * i)
            m = diff_attention(x, diff_wqkv[j], diff_wo[j], diff_lq1[j], diff_lk1[j], diff_lq2[j],
                               diff_lk2[j], diff_subln_g[j], rel_table, lam_init)
        else:
            m = forgetting_attention(x, fox_wqkv[j], fox_wo[j], fox_wf[j], fox_bf[j])
        x = layer_norm(ALPHA * x + m, ln_g[i, 1], ln_b[i, 1])
        x = layer_norm(ALPHA * x + 0.5 * swiglu(x, ffn2_wg[i], ffn2_wu[i], ffn2_wd[i]), ln_g[i, 2], ln_b[i, 2])
    return x
```

```python
from contextlib import ExitStack
import math
from concourse.bass_utils import run_bass_kernel_spmd
import numpy as np
import concourse.bass as bass
import concourse.mybir as mybir

F32, BF16 = mybir.dt.float32, mybir.dt.bfloat16
AF = mybir.ActivationFunctionType
ALU = mybir.AluOpType
AX = mybir.AxisListType


def _merge(dst, src):
    for k, v in src.items():
        if dst.get(k, (None, 0))[1] < v[1]:
            dst[k] = v


class Buf:
    def __init__(self, name):
        self.name = name
        self.prev = {}
        self.writers = {}
        self.readers = {}
        self.sem = None
        self.cnt = 0

    def new_gen(self):
        p = {}
        _merge(p, self.prev)
        _merge(p, self.writers)
        _merge(p, self.readers)
        self.prev = p
        self.writers = {}
        self.readers = {}


class KB:
    def __init__(self, nc, stack):
        self.nc = nc
        self.stack = stack
        self.engs = {"pe": nc.tensor, "act": nc.scalar, "dve": nc.vector, "pool": nc.gpsimd, "sp": nc.sync}
        self.esem = {}
        self.ecnt = {}
        for e in ("pe", "act", "dve", "pool"):
            self.esem[e] = stack.enter_context(nc.semaphore("s_" + e))
            self.ecnt[e] = 0
        self.waited = {e: {} for e in self.engs}
        self.dma_bufs = []
        self.sem_pool = []
        self.nsem = 4
        self.nwait = 0

    def buf(self, name):
        return Buf(name)

    def bufs(self, name, n):
        return [Buf(f"{name}{i}") for i in range(n)]

    def _sem_for(self, b):
        if b.sem is None:
            if self.sem_pool:
                b.sem, b.cnt = self.sem_pool.pop()
            else:
                b.sem = self.stack.enter_context(self.nc.semaphore("d_" + b.name))
                b.cnt = 0
                self.nsem += 1
            self.dma_bufs.append(b)
        return b.sem

    def release_sems(self, keep=()):
        rest = []
        for b in self.dma_bufs:
            if any(b is k for k in keep):
                rest.append(b)
            else:
                self.sem_pool.append((b.sem, b.cnt))
                b.sem = None
        self.dma_bufs = rest

    def _wait(self, eng, deps, skip_sem=None):
        w = self.waited[eng]
        for k, (sem, val) in deps.items():
            if skip_sem is not None and sem is skip_sem:
                continue
            if w.get(k, 0) < val:
                self.engs[eng].wait_ge(sem, val)
                w[k] = val
                self.nwait += 1

    def _deps(self, reads, writes, parts, rw):
        deps = {}
        for b in reads:
            _merge(deps, b.writers)
        for b in writes:
            b.new_gen()
            _merge(deps, b.prev)
        for b in parts:
            _merge(deps, b.prev)
        for b in rw:
            _merge(deps, b.prev)
            _merge(deps, b.writers)
            _merge(deps, b.readers)
        return deps

    def _update(self, ev, reads, writes, parts, rw):
        key = id(ev[0])
        for b in reads:
            _merge(b.readers, {key: ev})
        for b in list(writes) + list(parts):
            _merge(b.writers, {key: ev})
        for b in rw:
            b.prev = {}
            b.writers = {key: ev}
            b.readers = {}

    def op(self, eng, fn, reads=(), writes=(), parts=(), rw=()):
        deps = self._deps(reads, writes, parts, rw)
        self._wait(eng, deps, skip_sem=self.esem["pe"] if eng == "pe" else None)
        ins = fn(self.engs[eng])
        self.ecnt[eng] += 1
        ins.then_inc(self.esem[eng], 1)
        ev = (self.esem[eng], self.ecnt[eng])
        self._update(ev, reads, writes, parts, rw)
        return ins

    def dma(self, q, out, in_, sembuf, reads=(), writes=(), parts=(), rw=(), **kw):
        deps = self._deps(reads, writes, parts, rw)
        self._wait(q, deps)
        sem = self._sem_for(sembuf)
        ins = self.engs[q].dma_start(out=out, in_=in_, **kw)
        sembuf.cnt += 16
        ins.then_inc(sem, 16)
        ev = (sem, sembuf.cnt)
        self._update(ev, reads, writes, parts, rw)
        return ins

    def collective(self, kind, rg, in_ap, out_ap, sembuf, reads=(), writes=()):
        deps = self._deps(reads, writes, (), ())
        self._wait("pool", deps)
        sem = self._sem_for(sembuf)
        in_ap = in_ap.opt() if hasattr(in_ap, "opt") else in_ap
        out_ap = out_ap.opt() if hasattr(out_ap, "opt") else out_ap
        ins = self.engs["pool"].collective_compute(kind, ALU.bypass, replica_groups=rg, ins=[in_ap], outs=[out_ap])
        sembuf.cnt += 1
        ins.then_inc(sem)
        ev = (sem, sembuf.cnt)
        self._update(ev, reads, writes, (), ())
        return ins

    def all_events(self):
        deps = {}
        for e in ("pe", "act", "dve", "pool"):
            if self.ecnt[e] > 0:
                deps[id(self.esem[e])] = (self.esem[e], self.ecnt[e])
        for b in self.dma_bufs:
            if b.cnt > 0:
                deps[id(b.sem)] = (b.sem, b.cnt)
        return deps

    def barrier(self, engines=("pe", "act", "dve", "pool", "sp")):
        deps = self.all_events()
        for e in engines:
            self._wait(e, deps)

    def wait_all(self, eng):
        self._wait(eng, self.all_events())


ALPHA = 4 ** 0.25
LN_EPS = 1e-5
EPS_P = LN_EPS / (ALPHA * ALPHA)
NEG = -30000.0
RG = [[0, 1, 2, 3], [4, 5, 6, 7]]


class Cfg:
    def __init__(self, **kw):
        self.D = 2048; self.F = 5632; self.S = 16384; self.TB = 1024
        self.__dict__.update(kw)
        self.T = self.S // 4
        self.TB = min(self.TB, self.T)
        self.KC = self.D // 128
        self.FC = self.F // 128
        self.NJG = self.FC // 4
        self.NQT = self.S // 512
        self.NKT = self.S // 128
        self.rb = {}
        n = 0
        for l in range(2):
            for name, cnt in (("gu1", self.FC), ("d1", (4 * self.NJG) // 2), ("qkv", 24), ("wo", 8),
                              ("gu2", self.FC), ("d2", (4 * self.NJG) // 2)):
                self.rb[(l, name)] = n
                n += cnt
        self.NRB = ((n + 3) // 4) * 4
        self.K = self.NRB // 4


def barrier(kb, exclude=()):
    deps = kb.all_events()
    for b in exclude:
        if b.sem is not None:
            deps.pop(id(b.sem), None)
    for e in ("pe", "act", "dve", "pool", "sp"):
        kb._wait(e, deps)


class Prog:
    def __init__(self, nc, kb, cfg):
        self.nc, self.kb, self.cfg = nc, kb, cfg
        self.uid = 0
        self.AGW = None
        self.rb_off = 0

    def name(self, s):
        self.uid += 1
        return f"{s}_{self.uid}"

    def sb(self, st, n, shape, dt):
        return st.enter_context(self.nc.sbuf_tensor(self.name(n), shape, dt))

    def ps(self, st, n, shape, dt):
        return st.enter_context(self.nc.psum_tensor(self.name(n), shape, dt))

    def bar(self):
        keep = [self.AGW] if self.AGW is not None else []
        barrier(self.kb, exclude=keep)
        self.kb.release_sems(keep=keep)

    def need_rb(self, eng, rb_end):
        cnt = min((rb_end + self.rb_off + 3) // 4, self.AGW.cnt)
        self.kb._wait(eng, {id(self.AGW.sem): (self.AGW.sem, cnt)})


def phase0_weights(P, wshard, shard_bf, blobs):
    nc, kb, cfg = P.nc, P.kb, P.cfg
    P.AGW = kb.buf("AGW")
    with ExitStack() as st:
        f = [P.sb(st, "w0f", [128, 4096], F32) for _ in range(2)]; Fb = kb.bufs(P.name("W0F"), 2)
        b = [P.sb(st, "w0b", [128, 4096], BF16) for _ in range(2)]; Bb = kb.bufs(P.name("W0B"), 2)
        for k in range(cfg.K):
            s = k % 2
            kb.dma("sp", f[s][:], wshard[k], Fb[s], writes=[Fb[s]])
            kb.op("dve", lambda e: e.tensor_copy(out=b[s][:], in_=f[s][:]), reads=[Fb[s]], writes=[Bb[s]])
            SH = kb.buf(P.name("SH"))
            kb.dma("pool", shard_bf[k], b[s][:], Bb[s], reads=[Bb[s]], writes=[SH])
            KL = cfg.K // 2
            kb.collective("AllGather", RG, shard_bf[k], blobs[k // KL][(k % KL) * 512:(k % KL + 1) * 512, :], P.AGW, reads=[SH])
        P.bar()


def ln_sweep(P, st, y_scr, res_src, g_row, b_row, res_dst, xT_dst, r0, ntiles, ident_bf, IDB, plain=False):
    nc, kb, cfg = P.nc, P.kb, P.cfg
    D, KC = cfg.D, cfg.KC
    SD, AD = nc.vector.BN_STATS_DIM, nc.vector.BN_AGGR_DIM
    NCH = D // 512
    x_t = [P.sb(st, "x_t", [128, D], F32) for _ in range(2)]; XT = kb.bufs(P.name("XT"), 2)
    xb = [P.sb(st, "xb", [128, D], BF16) for _ in range(2)]; XB = kb.bufs(P.name("XB"), 2)
    xTo = [P.sb(st, "xTo", [128, KC, 512], BF16) for _ in range(2)]; XTO = kb.bufs(P.name("XTO"), 2)
    pT = [P.ps(st, "pT", [128, D], BF16) for _ in range(2)]; PT = kb.bufs(P.name("PT"), 2)
    if not plain:
        y_t = [P.sb(st, "y_t", [128, D], F32) for _ in range(2)]; YT = kb.bufs(P.name("YT"), 2)
        g_t = P.sb(st, "g_t", [128, D], F32); GT = kb.buf(P.name("GT"))
        b_t = P.sb(st, "b_t", [128, D], F32); BT = kb.buf(P.name("BT"))
        stt = [P.sb(st, "stt", [128, NCH, SD], F32) for _ in range(2)]; STT = kb.bufs(P.name("STT"), 2)
        mv = [P.sb(st, "mv", [128, AD + 2], F32) for _ in range(2)]; MV = kb.bufs(P.name("MV"), 2)
        kb.dma("sp", g_t[:], g_row.partition_broadcast(128), GT, writes=[GT])
        kb.dma("sp", b_t[:], b_row.partition_broadcast(128), BT, writes=[BT])

    def load(i):
        s = i % 2
        r = r0 + i * 128
        if not plain:
            kb.dma("sp", y_t[s][:], y_scr[r:r + 128, :], YT[s], writes=[YT[s]])
        kb.dma("sp", x_t[s][:], res_src[r:r + 128, :], XT[s], writes=[XT[s]])

    load(0)
    for i in range(ntiles):
        s = i % 2
        if i + 1 < ntiles:
            load(i + 1)
        r = r0 + i * 128
        xt = x_t[s]
        if not plain:
            yt = y_t[s]
            kb.op("dve", lambda e: e.tensor_tensor(out=yt[:], in0=yt[:], in1=xt[:], op=ALU.add), reads=[XT[s]], rw=[YT[s]])
            STT[s].new_gen()
            for c in range(NCH):
                kb.op("dve", lambda e, c=c: e.bn_stats(out=stt[s][:, c, :], in_=yt[:, c * 512:(c + 1) * 512]),
                      reads=[YT[s]], parts=[STT[s]])
            m = mv[s]
            kb.op("dve", lambda e: e.bn_aggr(out=m[:, 0:AD], in_=stt[s][:]), reads=[STT[s]], writes=[MV[s]])
            kb.op("act", lambda e: e.activation(out=m[:, AD:AD + 1], in_=m[:, 1:2], func=AF.Ln, bias=EPS_P), rw=[MV[s]])
            kb.op("act", lambda e: e.activation(out=m[:, AD:AD + 1], in_=m[:, AD:AD + 1], func=AF.Exp, scale=-0.5), rw=[MV[s]])
            kb.op("dve", lambda e: e.scalar_tensor_tensor(out=m[:, AD + 1:AD + 2], in0=m[:, 0:1], scalar=-1.0,
                                                          in1=m[:, AD:AD + 1], op0=ALU.mult, op1=ALU.mult), rw=[MV[s]])
            kb.op("act", lambda e: e.activation(out=xt[:], in_=yt[:], func=AF.Identity, bias=m[:, AD + 1:AD + 2],
                                                scale=m[:, AD:AD + 1]), reads=[YT[s], MV[s]], writes=[XT[s]])
            kb.op("pool", lambda e: e.tensor_tensor(out=xt[:], in0=xt[:], in1=g_t[:], op=ALU.mult), reads=[GT], rw=[XT[s]])
            kb.op("pool", lambda e: e.tensor_tensor(out=xt[:], in0=xt[:], in1=b_t[:], op=ALU.add), reads=[BT], rw=[XT[s]])
            kb.dma("pool", res_dst[r:r + 128, :], xt[:], XT[s], reads=[XT[s]])
        if xT_dst is not None:
            kb.op("dve", lambda e: e.tensor_copy(out=xb[s][:], in_=xt[:]), reads=[XT[s]], writes=[XB[s]])
            PT[s].new_gen()
            for c in range(KC):
                kb.op("pe", lambda e, c=c: e.transpose(pT[s][:, c * 128:(c + 1) * 128], xb[s][:, c * 128:(c + 1) * 128],
                                                       ident_bf[:]), reads=[XB[s], IDB], parts=[PT[s]])
            g4, q4 = divmod(i, 4)
            so = g4 % 2
            if q4 == 0:
                XTO[so].new_gen()
            kb.op("act", lambda e: e.activation(out=xTo[so][:, :, q4 * 128:(q4 + 1) * 128],
                                                in_=pT[s][:].rearrange("p (c t) -> p c t", t=128), func=AF.Copy),
                  reads=[PT[s]], parts=[XTO[so]])
            if q4 == 3 or i == ntiles - 1:
                w = (q4 + 1) * 128
                c0 = r0 + g4 * 512
                kb.dma("pool", xT_dst[:, c0:c0 + w].rearrange("(c p) t -> p c t", p=128), xTo[so][:, :, 0:w],
                       XTO[so], reads=[XTO[so]])


def ffn_phase(P, blob, rb_gu, rb_d, g_row, b_row, xT_src, res_src, res_dst, xT_dst, y_scr, ident_bf, IDB):
    nc, kb, cfg = P.nc, P.kb, P.cfg
    D, KC, FC, T, TB, NJG = cfg.D, cfg.KC, cfg.FC, cfg.T, cfg.TB, cfg.NJG
    NPASS, NH, NTT = T // TB, TB // 512, TB // 128
    NCB = D // 512
    yscale = 0.5 / ALPHA
    P.need_rb("sp", rb_d + (NCB * NJG) // 2)
    with ExitStack() as st:
        xT = P.sb(st, "xT", [128, KC, TB], BF16); XTB = kb.buf(P.name("XTB"))
        wgb = [P.sb(st, "wgb", [128, 4096], BF16) for _ in range(3)]; WGB = kb.bufs(P.name("WGB"), 3)
        wdb = [P.sb(st, "wdb", [128, 2048], BF16) for _ in range(3)]; WDB = kb.bufs(P.name("WDB"), 3)
        yb = [P.sb(st, "yb", [128, 512], F32) for _ in range(4)]; YB = kb.bufs(P.name("YB"), 4)
        sg = [P.sb(st, "sg", [128, 512], F32) for _ in range(2)]; SG = kb.bufs(P.name("SG"), 2)
        for ps_i in range(NPASS):
            t0 = ps_i * TB
            with ExitStack() as st2:
                hT = P.sb(st2, "hT", [128, FC * TB], BF16); HT = kb.buf(P.name("HT"))
                acc = [P.ps(st2, "acc", [128, 512], F32) for _ in range(8)]; ACC = kb.bufs(P.name("ACC"), 8)
                kb.dma("sp", xT[:], xT_src[:, t0:t0 + TB].rearrange("(c p) t -> p c t", p=128), XTB, writes=[XTB])
                tasks = [("gu", j) for j in range(FC)] + [("d", cb, jg) for cb in range(NCB) for jg in range(NJG)]

                def load(task):
                    if task[0] == "gu":
                        j = task[1]; s = j % 3
                        r = (rb_gu + j) * 128
                        kb.dma("sp", wgb[s][:], blob[r:r + 128, :], WGB[s], writes=[WGB[s]])
                    else:
                        _, cb, jg = task; tl = cb * NJG + jg; s = tl % 3
                        r = (rb_d + tl // 2) * 128
                        kb.dma("sp", wdb[s][:], blob[r:r + 128, (tl % 2) * 2048:(tl % 2 + 1) * 2048], WDB[s], writes=[WDB[s]])

                load(tasks[0]); load(tasks[1])
                ev_i = 0
                for ti, task in enumerate(tasks):
                    if ti + 2 < len(tasks):
                        load(tasks[ti + 2])
                    if task[0] == "gu":
                        j = task[1]; s = j % 3
                        for hb in range(NH):
                            k = (j * NH + hb) % 2
                            gp, up = acc[2 * k], acc[2 * k + 1]
                            GP, UP = ACC[2 * k], ACC[2 * k + 1]
                            GP.new_gen(); UP.new_gen()
                            for c in range(KC):
                                kb.op("pe", lambda e, c=c: e.matmul(gp[:], lhsT=wgb[s][:, c * 128:(c + 1) * 128],
                                                                    rhs=xT[:, c, hb * 512:(hb + 1) * 512],
                                                                    start=(c == 0), stop=(c == KC - 1)),
                                      reads=[WGB[s], XTB], parts=[GP])
                            for c in range(KC):
                                kb.op("pe", lambda e, c=c: e.matmul(up[:], lhsT=wgb[s][:, (KC + c) * 128:(KC + c + 1) * 128],
                                                                    rhs=xT[:, c, hb * 512:(hb + 1) * 512],
                                                                    start=(c == 0), stop=(c == KC - 1)),
                                      reads=[WGB[s], XTB], parts=[UP])
                            kb.op("act", lambda e: e.activation(out=sg[k][:], in_=gp[:], func=AF.Silu), reads=[GP], writes=[SG[k]])
                            o0 = j * TB + hb * 512
                            kb.op("dve", lambda e: e.tensor_tensor(out=hT[:, o0:o0 + 512], in0=sg[k][:], in1=up[:], op=ALU.mult),
                                  reads=[SG[k], UP], parts=[HT])
                    else:
                        _, cb, jg = task; s = (cb * NJG + jg) % 3
                        for jj in range(4):
                            j = jg * 4 + jj
                            for tt in range(NTT):
                                if j == 0:
                                    ACC[tt].new_gen()
                                kb.op("pe", lambda e, tt=tt, j=j, jj=jj: e.matmul(
                                    acc[tt][:], lhsT=hT[:, j * TB + tt * 128:j * TB + (tt + 1) * 128],
                                    rhs=wdb[s][:, jj * 512:(jj + 1) * 512], start=(j == 0), stop=(j == FC - 1)),
                                    reads=[WDB[s], HT], parts=[ACC[tt]])
                        if jg == NJG - 1:
                            for tt in range(NTT):
                                ys = ev_i % 4; ev_i += 1
                                kb.op("act", lambda e, tt=tt, ys=ys: e.activation(out=yb[ys][:], in_=acc[tt][:], func=AF.Identity, scale=yscale),
                                      reads=[ACC[tt]], writes=[YB[ys]])
                                r = t0 + tt * 128
                                kb.dma("pool", y_scr[r:r + 128, cb * 512:(cb + 1) * 512], yb[ys][:], YB[ys], reads=[YB[ys]])
                P.bar()
            with ExitStack() as st3:
                ln_sweep(P, st3, y_scr, res_src, g_row, b_row, res_dst, xT_dst, t0, NTT, ident_bf, IDB)
                P.bar()


def gather_chunks(P, srcs, dsts):
    kb = P.kb
    AGX = kb.buf(P.name("AGX"))
    for s_ap, d_ap in zip(srcs, dsts):
        kb.collective("AllGather", RG, s_ap, d_ap, AGX)
    P.bar()


def qkv_phase(P, blob, rb_qkv, xg_dst, qT_scr, kT_scr, v_scr, fox, wf_own, bf_own, lf_scr, pid_li):
    nc, kb, cfg = P.nc, P.kb, P.cfg
    KC, S, T = cfg.KC, cfg.S, cfg.T
    NB = S // 512
    qscale = 128 ** -0.5
    P.need_rb("pool", rb_qkv + 24)
    with ExitStack() as st:
        w = P.sb(st, "wqkv", [128, KC * 1536], BF16); W = kb.buf(P.name("WQKV"))
        WOWN = kb.buf(P.name("WOWN"))
        pid_li = nc.gpsimd.partition_id() % 4
        kb.dma("pool", P.wown[:, :], blob[bass.ds(pid_li * 768 + rb_qkv * 128, 768), :], WOWN, writes=[WOWN])
        for m in range(6):
            kb.dma("pool", w[:, m * 4096:(m + 1) * 4096], P.wown[m * 128:(m + 1) * 128, :], W, reads=[WOWN], parts=[W])
        xT = [P.sb(st, "qx", [128, KC, 512], BF16) for _ in range(2)]; XTB = kb.bufs(P.name("QX"), 2)
        qk_o = [P.sb(st, "qko", [128, 8, 512], BF16) for _ in range(2)]; QKO = kb.bufs(P.name("QKO"), 2)
        v_o = [P.sb(st, "vo", [128, 4, 512], BF16) for _ in range(2)]; VO = kb.bufs(P.name("VO"), 2)
        acc = [P.ps(st, "qacc", [128, 512], F32) for _ in range(4)]; ACC = kb.bufs(P.name("QACC"), 4)
        if fox:
            wff = P.sb(st, "wff", [128, KC * 4], F32); WFF = kb.buf(P.name("WFF"))
            wfb = P.sb(st, "wfb", [128, KC * 4], BF16); WFB = kb.buf(P.name("WFB"))
            bfc = P.sb(st, "bfc", [4, 2], F32); BFC = kb.buf(P.name("BFC"))
            kb.dma("sp", wff[:], wf_own[:, :], WFF, writes=[WFF])
            kb.dma("sp", bfc[:, 0:1], bf_own[:, :], BFC, writes=[BFC])
            kb.op("dve", lambda e: e.tensor_copy(out=wfb[:], in_=wff[:]), reads=[WFF], writes=[WFB])
            kb.op("dve", lambda e: e.tensor_scalar(out=bfc[:, 1:2], in0=bfc[:, 0:1], scalar1=-1.0, scalar2=None, op0=ALU.mult), rw=[BFC])
            facc = P.ps(st, "facc", [4, 512], F32); FACC = kb.buf(P.name("FACC"))
            lf = [P.sb(st, "lf", [4, 512], F32) for _ in range(2)]; LF = kb.bufs(P.name("LF"), 2)

        def load(b):
            s = b % 2
            r, t0 = divmod(b * 512, T)
            kb.dma("sp", xT[s][:], xg_dst[:, r * 128:(r + 1) * 128, t0:t0 + 512].rearrange("c p t -> p c t"), XTB[s],
                   writes=[XTB[s]])

        load(0)
        ai = 0
        for b in range(NB):
            s = b % 2
            if b + 1 < NB:
                load(b + 1)
            x = xT[s]
            QKO[s].new_gen()
            for o in range(8):
                a = ai % 4; ai += 1
                ACC[a].new_gen()
                for c in range(KC):
                    kb.op("pe", lambda e, c=c: e.matmul(acc[a][:], lhsT=w[:, c * 1536 + o * 128:c * 1536 + (o + 1) * 128],
                                                        rhs=x[:, c, :], start=(c == 0), stop=(c == KC - 1)),
                          reads=[W, XTB[s]], parts=[ACC[a]])
                eng = "act" if o % 2 == 0 else "dve"
                sc = qscale if o < 4 else 1.0
                if eng == "act":
                    kb.op("act", lambda e: e.activation(out=qk_o[s][:, o, :], in_=acc[a][:], func=AF.Identity, scale=sc),
                          reads=[ACC[a]], parts=[QKO[s]])
                else:
                    kb.op("dve", lambda e: e.tensor_scalar(out=qk_o[s][:, o, :], in0=acc[a][:], scalar1=sc, scalar2=None, op0=ALU.mult),
                          reads=[ACC[a]], parts=[QKO[s]])
            kb.dma("pool", qT_scr[:, :, b * 512:(b + 1) * 512].rearrange("u p t -> p u t"), qk_o[s][:, 0:4, :], QKO[s], reads=[QKO[s]])
            kb.dma("pool", kT_scr[:, :, b * 512:(b + 1) * 512].rearrange("u p t -> p u t"), qk_o[s][:, 4:8, :], QKO[s], reads=[QKO[s]])
            VO[s].new_gen()
            for tt in range(4):
                a = ai % 4; ai += 1
                ACC[a].new_gen()
                for c in range(KC):
                    kb.op("pe", lambda e, c=c: e.matmul(acc[a][:], lhsT=x[:, c, tt * 128:(tt + 1) * 128],
                                                        rhs=w[:, c * 1536 + 1024:c * 1536 + 1536], start=(c == 0), stop=(c == KC - 1)),
                          reads=[W, XTB[s]], parts=[ACC[a]])
                if tt % 2 == 0:
                    kb.op("act", lambda e: e.activation(out=v_o[s][:, tt, :], in_=acc[a][:], func=AF.Copy), reads=[ACC[a]], parts=[VO[s]])
                else:
                    kb.op("dve", lambda e: e.tensor_copy(out=v_o[s][:, tt, :], in_=acc[a][:]), reads=[ACC[a]], parts=[VO[s]])
            kb.dma("pool", v_scr[b * 512:(b + 1) * 512, :].rearrange("(tt p) e -> p tt e", p=128), v_o[s][:], VO[s], reads=[VO[s]])
            if fox:
                FACC.new_gen()
                for c in range(KC):
                    kb.op("pe", lambda e, c=c: e.matmul(facc[:], lhsT=wfb[:, c * 4:(c + 1) * 4], rhs=x[:, c, :],
                                                        start=(c == 0), stop=(c == KC - 1)), reads=[WFB, XTB[s]], parts=[FACC])
                kb.op("act", lambda e: e.activation(out=lf[s][:], in_=facc[:], func=AF.Exp, bias=bfc[:, 1:2], scale=-1.0),
                      reads=[FACC, BFC], writes=[LF[s]])
                kb.op("act", lambda e: e.activation(out=lf[s][:], in_=lf[s][:], func=AF.Ln, bias=1.0), rw=[LF[s]])
                kb.dma("pool", lf_scr[:, b * 512:(b + 1) * 512], lf[s][:], LF[s], reads=[LF[s]])
        P.bar()


def fox_cumsum(P, lf_scr, negc_scr, mt_dram):
    nc, kb, cfg = P.nc, P.kb, P.cfg
    SEG = cfg.S // 32
    with ExitStack() as st:
        L = P.sb(st, "cL", [128, SEG], F32); LB = kb.buf(P.name("CL"))
        Z = P.sb(st, "cZ", [128, SEG], F32); ZB = kb.buf(P.name("CZ"))
        C = P.sb(st, "cC", [128, SEG], F32); CB = kb.buf(P.name("CC"))
        MT = P.sb(st, "cMT", [128, 128], F32); MTB = kb.buf(P.name("CMT"))
        off = P.sb(st, "coff", [128, 1], F32); OFF = kb.buf(P.name("COFF"))
        pp = P.ps(st, "cpp", [128, 2], F32); PP = kb.buf(P.name("CPP"))
        kb.dma("sp", L[:], lf_scr.rearrange("u (g s) -> (u g) s", s=SEG), LB, writes=[LB])
        kb.dma("sp", MT[:], mt_dram[:, :], MTB, writes=[MTB])
        kb.op("dve", lambda e: e.memset(Z[:], 0.0), writes=[ZB])
        kb.op("dve", lambda e: e.tensor_tensor_scan(out=C[:], data0=Z[:], data1=L[:], initial=0.0, op0=ALU.add, op1=ALU.add),
              reads=[ZB, LB], writes=[CB])
        kb.op("pe", lambda e: e.matmul(pp[:, 0:1], lhsT=MT[:, :], rhs=C[:, SEG - 1:SEG], start=True, stop=True),
              reads=[MTB, CB], writes=[PP])
        kb.op("dve", lambda e: e.tensor_copy(out=off[:], in_=pp[:, 0:1]), reads=[PP], writes=[OFF])
        kb.op("dve", lambda e: e.tensor_scalar(out=C[:], in0=C[:], scalar1=off[:, 0:1], scalar2=None, op0=ALU.add), reads=[OFF], rw=[CB])
        kb.dma("pool", negc_scr.rearrange("u (g s) -> (u g) s", s=SEG), C[:], CB, reads=[CB])
        P.bar()


def attention_phase(P, fox, qT_scr, kT_scr, v_scr, oT_src, ident_bf, IDB, ident_f, IDF, tri_dram, negc_scr,
                    relm, lqk, subg, lam_init):
    nc, kb, cfg = P.nc, P.kb, P.cfg
    S, T, NQT, NKT = cfg.S, cfg.T, cfg.NQT, cfg.NKT
    VW = 128 if fox else 256
    NHEAD = 4 if fox else 2
    with ExitStack() as st:
        KTs = [P.sb(st, "aKT", [128, S], BF16) for _ in range(1 if fox else 2)]; KTBs = kb.bufs(P.name("AKT"), 1 if fox else 2)
        V = P.sb(st, "aV", [128, NKT, VW + 1], BF16); VB = kb.buf(P.name("AV"))
        QT = [P.sb(st, "aQT", [128, 512], BF16) for _ in range(2)]; QTB = kb.bufs(P.name("AQT"), 2)
        PTl = [P.sb(st, "aP", [128, 512], BF16) for _ in range(4)]; PB = kb.bufs(P.name("AP"), 4)
        TMP = [P.sb(st, "aTMP", [128, 512], F32) for _ in range(2)]; TMPB = kb.bufs(P.name("ATMP"), 2)
        sps = [P.ps(st, "aS", [128, 512], F32) for _ in range(2)]; SPS = kb.bufs(P.name("AS"), 2)
        ops_ = [P.ps(st, "aO", [128, 512], F32) for _ in range(4)]; OPS = kb.bufs(P.name("AO"), 4)
        tps = P.ps(st, "aT", [128, 1024], BF16); TPS = kb.buf(P.name("AT"))
        osb = [P.sb(st, "aosb", [128, VW + 1], F32) for _ in range(2)]; OSB = kb.bufs(P.name("AOSB"), 2)
        rl = [P.sb(st, "arl", [128, 4], F32) for _ in range(2)]; RL = kb.bufs(P.name("ARL"), 2)
        onb = [P.sb(st, "aonb", [128, VW], BF16) for _ in range(2)]; ONB = kb.bufs(P.name("AONB"), 2)
        oT = [P.sb(st, "aoT", [128, VW // 128, 512], BF16) for _ in range(2)]; OTB = kb.bufs(P.name("AOT"), 2)
        if fox:
            tri_f = P.sb(st, "atrif", [128, 128], F32); TRIF = kb.buf(P.name("ATRIF"))
            tri = P.sb(st, "atri", [128, 128], BF16); TRI = kb.buf(P.name("ATRI"))
            kb.dma("sp", tri_f[:], tri_dram[:, :], TRIF, writes=[TRIF])
            kb.op("dve", lambda e: e.tensor_copy(out=tri[:], in_=tri_f[:]), reads=[TRIF], writes=[TRI])
            ncr = P.sb(st, "ancr", [NKT, 128], F32); NCR = kb.buf(P.name("ANCR"))
            negck = P.sb(st, "anegck", [128, NKT], F32); NEGCK = kb.buf(P.name("ANEGCK"))
            cbc = [P.sb(st, "acbc", [128, 512], F32) for _ in range(2)]; CBC = kb.bufs(P.name("ACBC"), 2)
            ncp = P.ps(st, "ancp", [128, NKT], F32); NCP = kb.buf(P.name("ANCP"))
        else:
            M = P.sb(st, "aM", [128, 1024], F32); MB = kb.buf(P.name("AM"))
            o1n = P.sb(st, "ao1n", [128, 4, 256], F32); O1N = kb.bufs(P.name("AO1N"), 4)
            od = [P.sb(st, "aod", [128, 256], F32) for _ in range(2)]; OD = kb.bufs(P.name("AOD"), 2)
            sqj = P.sb(st, "asqj", [128, 256], F32); SQJ = kb.buf(P.name("ASQJ"))
            lq = P.sb(st, "alq", [128, 4, 128], F32); LQ = kb.buf(P.name("ALQ"))
            lam = P.sb(st, "alam", [128, 8], F32); LAM = kb.buf(P.name("ALAM"))
            gsb = P.sb(st, "agsb", [128, 256], F32); GSB = kb.buf(P.name("AGSB"))
            for i in range(4):
                kb.dma("sp", lq[:, i, :], lqk[i:i + 1, :].partition_broadcast(128), LQ, parts=[LQ])
            kb.dma("sp", gsb[:], subg[0:1, :].partition_broadcast(128), GSB, writes=[GSB])
            kb.op("dve", lambda e: e.tensor_tensor(out=lq[:, 0, :], in0=lq[:, 0, :], in1=lq[:, 1, :], op=ALU.mult), rw=[LQ])
            kb.op("dve", lambda e: e.tensor_tensor(out=lq[:, 2, :], in0=lq[:, 2, :], in1=lq[:, 3, :], op=ALU.mult), rw=[LQ])
            kb.op("dve", lambda e: e.reduce_sum(out=lam[:, 0:1], in_=lq[:, 0, :], axis=AX.X), reads=[LQ], writes=[LAM])
            kb.op("dve", lambda e: e.reduce_sum(out=lam[:, 1:2], in_=lq[:, 2, :], axis=AX.X), reads=[LQ], rw=[LAM])
            kb.op("act", lambda e: e.activation(out=lam[:, 2:4], in_=lam[:, 0:2], func=AF.Exp), rw=[LAM])
            kb.op("dve", lambda e: e.tensor_tensor(out=lam[:, 4:5], in0=lam[:, 3:4], in1=lam[:, 2:3], op=ALU.subtract), rw=[LAM])
            kb.op("dve", lambda e: e.tensor_scalar(out=lam[:, 5:6], in0=lam[:, 4:5], scalar1=-lam_init, scalar2=None, op0=ALU.add), rw=[LAM])
            kb.op("dve", lambda e: e.tensor_scalar(out=gsb[:], in0=gsb[:], scalar1=1.0 - lam_init, scalar2=None, op0=ALU.mult), rw=[GSB])
            neglam = lam[:, 5:6]
            ssum = [P.sb(st, "assum", [128, 2], F32) for _ in range(2)]; SSUM = kb.bufs(P.name("ASSUM"), 2)

        pi = 0
        si = 0
        fi = 0
        NM = 1 if fox else 2
        for head in range(NHEAD):
            for mp_ in range(NM):
                kb.dma("sp", KTs[mp_][:], kT_scr[head * NM + mp_], KTBs[mp_], writes=[KTBs[mp_]])
            VB.new_gen()
            vsrc = v_scr[:, head * VW:(head + 1) * VW].rearrange("(kt p) e -> p kt e", p=128)
            nvs = max(1, NKT // 32)
            for vi in range(nvs):
                k0, k1 = vi * (NKT // nvs), (vi + 1) * (NKT // nvs)
                kb.dma("sp", V[:, k0:k1, 0:VW], vsrc[:, k0:k1, :], VB, parts=[VB])
            kb.op("pool", lambda e: e.memset(V[:, :, VW:VW + 1], 1.0), parts=[VB])
            if fox:
                kb.dma("sp", ncr[:], negc_scr[head:head + 1, :].rearrange("o (kt p) -> (o kt) p", p=128), NCR, writes=[NCR])
                kb.op("pe", lambda e: e.transpose(ncp[:], ncr[:], ident_f[0:NKT, 0:NKT]), reads=[NCR, IDF], writes=[NCP])
                kb.op("dve", lambda e: e.tensor_copy(out=negck[:], in_=ncp[:]), reads=[NCP], writes=[NEGCK])
            else:
                kb.dma("sp", M[:], relm[head], MB, writes=[MB])
            for t, mp in [(t, mp) for t in range(NQT) for mp in range(NM)]:
                u = head * NM + mp
                KT, KTB = KTs[mp], KTBs[mp]
                qs_ = (t * NM + mp) % 2
                kb.dma("sp", QT[qs_][:], qT_scr[u, :, t * 512:(t + 1) * 512], QTB[qs_], writes=[QTB[qs_]])
                if fox:
                    kb.dma("sp", cbc[qs_][:], negc_scr[u:u + 1, t * 512:(t + 1) * 512].partition_broadcast(128), CBC[qs_], writes=[CBC[qs_]])
                for q in range(4):
                    OPS[q].new_gen()
                nk = 4 * t + 4
                for kt in range(nk):
                    j = kt - 4 * t
                    c0 = max(j, 0) * 128
                    sp_ = si % 2; si += 1
                    pp_ = pi % 4; pi += 1
                    kb.op("pe", lambda e: e.matmul(sps[sp_][:, c0:512], lhsT=KT[:, kt * 128:(kt + 1) * 128], rhs=QT[qs_][:, c0:512],
                                                   start=True, stop=True), reads=[KTB, QTB[qs_]], writes=[SPS[sp_]])
                    if fox:
                        tm = TMP[sp_]
                        kb.op("dve", lambda e: e.tensor_tensor(out=tm[:, c0:512], in0=sps[sp_][:, c0:512], in1=cbc[qs_][:, c0:512],
                                                               op=ALU.subtract), reads=[SPS[sp_], CBC[qs_]], writes=[TMPB[sp_]])
                        kb.op("act", lambda e: e.activation(out=PTl[pp_][:, c0:512], in_=tm[:, c0:512], func=AF.Exp,
                                                            bias=negck[:, kt:kt + 1]), reads=[TMPB[sp_], NEGCK], writes=[PB[pp_]])
                        if j >= 0:
                            kb.op("pool", lambda e: e.tensor_tensor(out=PTl[pp_][:, c0:c0 + 128], in0=PTl[pp_][:, c0:c0 + 128],
                                                                    in1=tri[:], op=ALU.mult), reads=[TRI], rw=[PB[pp_]])
                    else:
                        if j >= -1:
                            tm = TMP[sp_]
                            x0 = 384 - j * 128 + c0
                            kb.op("dve", lambda e: e.tensor_tensor(out=tm[:, c0:512], in0=sps[sp_][:, c0:512], in1=M[:, x0:x0 + 512 - c0],
                                                                   op=ALU.add), reads=[SPS[sp_], MB], writes=[TMPB[sp_]])
                            kb.op("act", lambda e: e.activation(out=PTl[pp_][:, c0:512], in_=tm[:, c0:512], func=AF.Exp),
                                  reads=[TMPB[sp_]], writes=[PB[pp_]])
                        else:
                            kb.op("act", lambda e: e.activation(out=PTl[pp_][:, :], in_=sps[sp_][:, :], func=AF.Exp,
                                                                bias=M[:, 1023:1024]), reads=[SPS[sp_], MB], writes=[PB[pp_]])
                    for q in range(max(j, 0), 4):
                        last = (kt == 4 * t + q)
                        kb.op("pe", lambda e, q=q: e.matmul(ops_[q][:, 0:VW + 1], lhsT=PTl[pp_][:, q * 128:(q + 1) * 128], rhs=V[:, kt, :],
                                                            start=(kt == 0), stop=last), reads=[PB[pp_], VB], parts=[OPS[q]])
                ot = fi % 2
                if fox or mp == 1:
                    OTB[ot].new_gen()
                for q in range(4):
                    f2 = (fi * 4 + q) % 2
                    ob, r_ = osb[f2], rl[f2]
                    kb.op("act", lambda e: e.activation(out=ob[:], in_=ops_[q][:, 0:VW + 1], func=AF.Copy), reads=[OPS[q]], writes=[OSB[f2]])
                    kb.op("dve", lambda e: e.reciprocal(out=r_[:, 0:1], in_=ob[:, VW:VW + 1]), reads=[OSB[f2]], writes=[RL[f2]])
                    if fox:
                        kb.op("dve", lambda e: e.tensor_scalar(out=onb[f2][:], in0=ob[:, 0:VW], scalar1=r_[:, 0:1], scalar2=None, op0=ALU.mult),
                              reads=[OSB[f2], RL[f2]], writes=[ONB[f2]])
                    elif mp == 0:
                        kb.op("dve", lambda e: e.tensor_scalar(out=o1n[:, q, :], in0=ob[:, 0:VW], scalar1=r_[:, 0:1], scalar2=None, op0=ALU.mult),
                              reads=[OSB[f2], RL[f2]], writes=[O1N[q]])
                        continue
                    else:
                        d_ = od[f2]
                        kb.op("dve", lambda e: e.tensor_scalar(out=r_[:, 1:2], in0=r_[:, 0:1], scalar1=neglam, scalar2=None, op0=ALU.mult),
                              reads=[LAM], rw=[RL[f2]])
                        kb.op("dve", lambda e: e.scalar_tensor_tensor(out=d_[:], in0=ob[:, 0:VW], scalar=r_[:, 1:2], in1=o1n[:, q, :],
                                                                      op0=ALU.mult, op1=ALU.add), reads=[OSB[f2], RL[f2], O1N[q]], writes=[OD[f2]])
                        ss_ = ssum[f2]
                        kb.op("act", lambda e: e.activation(out=sqj[:], in_=d_[:], func=AF.Square, accum_out=ss_[:, 0:1]),
                              reads=[OD[f2]], writes=[SQJ, SSUM[f2]])
                        kb.op("act", lambda e: e.activation(out=ss_[:, 1:2], in_=ss_[:, 0:1], func=AF.Ln, bias=LN_EPS, scale=1.0 / 256), rw=[SSUM[f2]])
                        kb.op("act", lambda e: e.activation(out=ss_[:, 1:2], in_=ss_[:, 1:2], func=AF.Exp, scale=-0.5), rw=[SSUM[f2]])
                        kb.op("dve", lambda e: e.scalar_tensor_tensor(out=onb[f2][:], in0=d_[:], scalar=ss_[:, 1:2], in1=gsb[:],
                                                                      op0=ALU.mult, op1=ALU.mult), reads=[OD[f2], SSUM[f2], GSB], writes=[ONB[f2]])
                    TPS.new_gen()
                    for c in range(VW // 128):
                        kb.op("pe", lambda e, c=c: e.transpose(tps[:, c * 128:(c + 1) * 128], onb[f2][:, c * 128:(c + 1) * 128], ident_bf[:]),
                              reads=[ONB[f2], IDB], parts=[TPS])
                    kb.op("act", lambda e: e.activation(out=oT[ot][:, :, q * 128:(q + 1) * 128],
                                                        in_=tps[:, 0:VW].rearrange("p (c t) -> p c t", t=128), func=AF.Copy),
                          reads=[TPS], parts=[OTB[ot]])
                if fox or mp == 1:
                    tc, tq = divmod(t * 512, T)
                    for c in range(VW // 128):
                        fc = u if fox else head * 2 + c
                        kb.dma("pool", oT_src[fc, tc, :, tq:tq + 512], oT[ot][:, c, :], OTB[ot], reads=[OTB[ot]])
                fi += 1
        P.bar()


def wo_phase(P, blob, rb_wo, oT_dst2, y_scr, pid_li):
    nc, kb, cfg = P.nc, P.kb, P.cfg
    KC, T, D = cfg.KC, cfg.T, cfg.D
    P.need_rb("sp", rb_wo + 8)
    with ExitStack() as st:
        w = P.sb(st, "wo", [128, KC * D], BF16); W = kb.buf(P.name("WO"))
        for m in range(8):
            r = (rb_wo + m) * 128
            kb.dma("sp", w[:, m * 4096:(m + 1) * 4096], blob[r:r + 128, :], W, parts=[W])
        oT = [P.sb(st, "woT", [128, KC, 512], BF16) for _ in range(2)]; OT = kb.bufs(P.name("WOT"), 2)
        acc = [P.ps(st, "wacc", [128, 512], F32) for _ in range(4)]; ACC = kb.bufs(P.name("WACC"), 4)
        yb = [P.sb(st, "wyb", [128, 512], F32) for _ in range(4)]; YB = kb.bufs(P.name("WYB"), 4)
        NG = T // 512

        OMINE = kb.buf(P.name("OMINE"))
        pid_sp = nc.sync.partition_id() % 4
        base = oT_dst2[bass.ds(pid_sp * 2048, 2048), :]
        for fc in range(4):
            kb.dma("sp", P.omine[fc], base[fc * 512:(fc + 1) * 512, :], OMINE, parts=[OMINE])

        def load(g):
            s = g % 2
            OT[s].new_gen()
            for fc in range(4):
                for r in range(4):
                    ci = r * 4 + fc
                    kb.dma("sp", oT[s][:, ci, :], P.omine[fc, r * 128:(r + 1) * 128, g * 512:(g + 1) * 512], OT[s],
                           reads=[OMINE], parts=[OT[s]])

        load(0)
        ai = 0
        for g in range(NG):
            s = g % 2
            if g + 1 < NG:
                load(g + 1)
            for tt in range(4):
                for cb in range(4):
                    a = ai % 4; ai += 1
                    ACC[a].new_gen()
                    for c in range(KC):
                        kb.op("pe", lambda e, c=c: e.matmul(acc[a][:], lhsT=oT[s][:, c, tt * 128:(tt + 1) * 128],
                                                            rhs=w[:, c * D + cb * 512:c * D + (cb + 1) * 512], start=(c == 0), stop=(c == KC - 1)),
                              reads=[W, OT[s]], parts=[ACC[a]])
                    kb.op("act", lambda e: e.activation(out=yb[a][:], in_=acc[a][:], func=AF.Identity, scale=1.0 / ALPHA),
                          reads=[ACC[a]], writes=[YB[a]])
                    r0 = g * 512 + tt * 128
                    kb.dma("pool", y_scr[r0:r0 + 128, cb * 512:(cb + 1) * 512], yb[a][:], YB[a], reads=[YB[a]])
        P.bar()


def build_program(cfg):
    nc = bass.Bass("TRN2", target_bir_lowering=False)
    D, T, S, K = cfg.D, cfg.T, cfg.S, cfg.K
    dt_in = lambda n, s: nc.dram_tensor(n, s, F32, kind="ExternalInput").ap()
    x_tm = dt_in("x_tm", [T, D])
    wshard = dt_in("wshard", [K, 128, 4096])
    gb = dt_in("gb", [12, D])
    relm = dt_in("relm", [2, 128, 1024])
    lqk = dt_in("lqk", [4, 128])
    subg = dt_in("subg", [1, 256])
    wf_own = dt_in("wf_own", [128, cfg.KC * 4])
    bf_own = dt_in("bf_own", [4, 1])
    ident = dt_in("ident", [128, 128])
    tri = dt_in("tri", [128, 128])
    mt = dt_in("mt", [128, 128])
    out = nc.dram_tensor("out", [T, D], F32, kind="ExternalOutput").ap()
    di = lambda n, s, d: nc.dram_tensor(n, s, d)
    RBL = cfg.NRB // 2
    assert RBL % 4 == 0 and cfg.NRB == 2 * cfg.rb[(1, "gu1")]
    blobs = [di(f"blob{l}", [RBL * 128, 4096], BF16) for l in range(2)]
    shard_bf = di("shard_bf", [K, 128, 4096], BF16)
    xT_A = di("xT_A", [D, T], BF16)
    xT_B = di("xT_B", [D, T], BF16)
    xg_src = di("xg_src", [D, T], BF16)
    xg_dst = di("xg_dst", [cfg.KC, 512, T], BF16)
    res = [di(f"res{i}", [T, D], F32) for i in range(3)]
    y_scr = di("y_scr", [T, D], F32)
    qT_scr = di("qT_scr", [4, 128, S], BF16)
    kT_scr = di("kT_scr", [4, 128, S], BF16)
    v_scr = di("v_scr", [S, 512], BF16)
    lf_scr = di("lf_scr", [4, S], F32)
    negc_scr = di("negc_scr", [4, S], F32)
    oT_src = di("oT_src", [4, 4, 128, T], BF16)
    oT_dst = di("oT_dst", [4, 4, 512, T], BF16)
    oT_dst2 = oT_dst.reshape([4 * 4 * 512, T])
    wown = di("wown", [768, 4096], BF16)
    omine = di("omine", [4, 512, T], BF16)
    with ExitStack() as st:
        kb = KB(nc, st)
        P = Prog(nc, kb, cfg)
        P.wown, P.omine = wown, omine
        st.enter_context(nc.Block())
        pid_li = nc.gpsimd.partition_id() % 4
        idf = P.sb(st, "idf", [128, 128], F32); IDF = kb.buf("IDF")
        idb = P.sb(st, "idb", [128, 128], BF16); IDB = kb.buf("IDB")
        kb.dma("sp", idf[:], ident[:, :], IDF, writes=[IDF])
        kb.op("dve", lambda e: e.tensor_copy(out=idb[:], in_=idf[:]), reads=[IDF], writes=[IDB])
        phase0_weights(P, wshard, shard_bf, blobs)
        with ExitStack() as st1:
            ln_sweep(P, st1, None, x_tm, None, None, None, xT_A, 0, T // 128, idb, IDB, plain=True)
            P.bar()
        r_in = x_tm
        for l in range(2):
            fox = (l == 1)
            rb = lambda n: cfg.rb[(l, n)] - l * RBL
            blob = blobs[l]
            P.rb_off = l * RBL
            grow = lambda i: (gb[(l * 3 + i) * 2:(l * 3 + i) * 2 + 1, :], gb[(l * 3 + i) * 2 + 1:(l * 3 + i) * 2 + 2, :])
            g_, b_ = grow(0)
            ffn_phase(P, blob, rb("gu1"), rb("d1"), g_, b_, xT_A, r_in, res[0], xg_src, y_scr, idb, IDB)
            gather_chunks(P, [xg_src[c * 128:(c + 1) * 128, :] for c in range(cfg.KC)], [xg_dst[c] for c in range(cfg.KC)])
            qkv_phase(P, blob, rb("qkv"), xg_dst, qT_scr, kT_scr, v_scr, fox, wf_own, bf_own, lf_scr, pid_li)
            if fox:
                fox_cumsum(P, lf_scr, negc_scr, mt)
            attention_phase(P, fox, qT_scr, kT_scr, v_scr, oT_src, idb, IDB, idf, IDF, tri, negc_scr, relm, lqk, subg,
                            0.8 - 0.6 * math.exp(-0.3 * l))
            gather_chunks(P, [oT_src[fc, tc] for tc in range(4) for fc in range(4)],
                          [oT_dst[tc, fc] for tc in range(4) for fc in range(4)])
            wo_phase(P, blob, rb("wo"), oT_dst2, y_scr, pid_li)
            g_, b_ = grow(1)
            with ExitStack() as st2:
                ln_sweep(P, st2, y_scr, res[0], g_, b_, res[1], xT_B, 0, T // 128, idb, IDB)
                P.bar()
            g_, b_ = grow(2)
            last = (l == 1)
            ffn_phase(P, blob, rb("gu2"), rb("d2"), g_, b_, xT_B, res[1], out if last else res[2], None if last else xT_A,
                      y_scr, idb, IDB)
            r_in = res[2]
        kb.wait_all("pool")
        kb.wait_all("sp")
        build_program.stats = dict(sems=kb.nsem, waits=kb.nwait, counts=dict(kb.ecnt))
    return nc


def _t5_bucket_np(n):
    n = np.maximum(n, 0)
    nf = np.maximum(n, 1).astype(np.float32)
    large = 16 + (np.log(nf / np.float32(16)) / np.float32(math.log(128 / 16)) * np.float32(16)).astype(np.int32)
    large = np.minimum(large, 31)
    return np.where(n < 16, n, large)


def _tile_gu(wg, wu, KC, FC):
    out = np.empty((FC, 128, 2, KC, 128), np.float32)
    for m, w in enumerate((wg, wu)):
        out[:, :, m] = w.reshape(KC, 128, FC, 128).transpose(2, 1, 0, 3)
    return out.reshape(FC, 128, 4096)


def _tile_d(wd, FC):
    t = wd.reshape(FC // 4, 4, 128, 4, 512).transpose(3, 0, 2, 1, 4).reshape(FC, 128, 2048)
    return np.ascontiguousarray(t.reshape(FC // 2, 2, 128, 2048).transpose(0, 2, 1, 3)).reshape(FC // 2, 128, 4096)


def _tile_rows(w, KC):
    N = w.shape[1]
    t = w.reshape(KC, 128, N).transpose(1, 0, 2).reshape(128, KC * N)
    return np.ascontiguousarray(t.reshape(128, KC * N // 4096, 4096).transpose(1, 0, 2))


def pack_inputs(cfg, inp):
    D, S, T, KC, FC = cfg.D, cfg.S, cfg.T, cfg.KC, cfg.FC
    f32 = lambda a: np.asarray(a, dtype=np.float32)
    rbs = np.zeros((cfg.NRB, 128, 4096), np.float32)
    for l in range(2):
        rbs[cfg.rb[(l, "gu1")]:cfg.rb[(l, "gu1")] + FC] = _tile_gu(f32(inp["ffn1_wg"][l]), f32(inp["ffn1_wu"][l]), KC, FC)
        rbs[cfg.rb[(l, "d1")]:cfg.rb[(l, "d1")] + FC // 2] = _tile_d(f32(inp["ffn1_wd"][l]), FC)
        rbs[cfg.rb[(l, "gu2")]:cfg.rb[(l, "gu2")] + FC] = _tile_gu(f32(inp["ffn2_wg"][l]), f32(inp["ffn2_wu"][l]), KC, FC)
        rbs[cfg.rb[(l, "d2")]:cfg.rb[(l, "d2")] + FC // 2] = _tile_d(f32(inp["ffn2_wd"][l]), FC)
        wqkv = f32(inp["diff_wqkv"][0] if l == 0 else inp["fox_wqkv"][0])
        wo = f32(inp["diff_wo"][0] if l == 0 else inp["fox_wo"][0])
        for r in range(4):
            own = np.concatenate([wqkv[:, r * 512:(r + 1) * 512], wqkv[:, D + r * 512:D + (r + 1) * 512],
                                  wqkv[:, 2 * D + r * 512:2 * D + (r + 1) * 512]], axis=1)
            b0 = cfg.rb[(l, "qkv")] + r * 6
            rbs[b0:b0 + 6] = _tile_rows(own, KC)
        rbs[cfg.rb[(l, "wo")]:cfg.rb[(l, "wo")] + 8] = _tile_rows(wo, KC)
    shards = [np.ascontiguousarray(rbs[r::4]) for r in range(4)]
    gb = np.empty((12, D), np.float32)
    for l in range(2):
        for i in range(3):
            gb[(l * 3 + i) * 2] = inp["ln_g"][l, i]
            gb[(l * 3 + i) * 2 + 1] = inp["ln_b"][l, i]
    kp = np.arange(128)[:, None]; xx = np.arange(1024)[None, :]
    n = xx - kp - 384
    bucket = _t5_bucket_np(n)
    rel = f32(inp["rel_table"])
    ident = np.eye(128, dtype=np.float32)
    tri = (np.arange(128)[None, :] >= np.arange(128)[:, None]).astype(np.float32)
    pp = np.arange(128)
    mt = ((pp[:, None] // 32 == pp[None, :] // 32) & (pp[:, None] < pp[None, :])).astype(np.float32)
    lqk = np.stack([f32(inp["diff_lq1"][0]), f32(inp["diff_lk1"][0]), f32(inp["diff_lq2"][0]), f32(inp["diff_lk2"][0])])
    subg = f32(inp["diff_subln_g"][0]).reshape(1, 256)
    wf = f32(inp["fox_wf"][0]); bfv = f32(inp["fox_bf"][0])
    x = inp["x"]
    in_maps = []
    for core in range(8):
        b, li = divmod(core, 4)
        relm = np.empty((2, 128, 1024), np.float32)
        for hl in range(2):
            relm[hl] = np.where(n >= 0, rel[bucket, 2 * li + hl], np.float32(NEG))
        wf_own = np.ascontiguousarray(wf[:, 4 * li:4 * li + 4].reshape(KC, 128, 4).transpose(1, 0, 2)).reshape(128, KC * 4)
        in_maps.append({
            "x_tm": np.ascontiguousarray(f32(x[b, li * T:(li + 1) * T, :])),
            "wshard": shards[li], "gb": gb, "relm": relm, "lqk": lqk, "subg": subg,
            "wf_own": wf_own, "bf_own": np.ascontiguousarray(bfv[4 * li:4 * li + 4].reshape(4, 1)),
            "ident": ident, "tri": tri, "mt": mt,
        })
    return in_maps


def run(cfg, inp, trace=False):
    nc = build_program(cfg)
    in_maps = pack_inputs(cfg, inp)
    res = run_bass_kernel_spmd(nc, in_maps, core_ids=list(range(8)))
    out = np.empty((2, cfg.S, cfg.D), np.float32)
    for core in range(8):
        b, li = divmod(core, 4)
        out[b, li * cfg.T:(li + 1) * cfg.T] = res.results[core]["out"]
    return out


def kernel(**inputs):
    cfg = Cfg()
    return run(cfg, inputs)
```

```python
from contextlib import ExitStack
import math
from concourse.bass_utils import run_bass_kernel_spmd
import numpy as np
import concourse.bass as bass
import concourse.mybir as mybir

F32, BF16 = mybir.dt.float32, mybir.dt.bfloat16
AF = mybir.ActivationFunctionType
ALU = mybir.AluOpType
AX = mybir.AxisListType


def _merge(dst, src):
    for k, v in src.items():
        if dst.get(k, (None, 0))[1] < v[1]:
            dst[k] = v


class Buf:
    def __init__(self, name):
        self.name = name
        self.prev = {}
        self.writers = {}
        self.readers = {}
        self.sem = None
        self.cnt = 0

    def new_gen(self):
        p = {}
        _merge(p, self.prev)
        _merge(p, self.writers)
        _merge(p, self.readers)
        self.prev = p
        self.writers = {}
        self.readers = {}


class KB:
    def __init__(self, nc, stack):
        self.nc = nc
        self.stack = stack
        self.engs = {"pe": nc.tensor, "act": nc.scalar, "dve": nc.vector, "pool": nc.gpsimd, "sp": nc.sync}
        self.esem = {}
        self.ecnt = {}
        for e in ("pe", "act", "dve", "pool"):
            self.esem[e] = stack.enter_context(nc.semaphore("s_" + e))
            self.ecnt[e] = 0
        self.waited = {e: {} for e in self.engs}
        self.dma_bufs = []
        self.sem_pool = []
        self.nsem = 4
        self.nwait = 0

    def buf(self, name):
        return Buf(name)

    def bufs(self, name, n):
        return [Buf(f"{name}{i}") for i in range(n)]

    def _sem_for(self, b):
        if b.sem is None:
            if self.sem_pool:
                b.sem, b.cnt = self.sem_pool.pop()
            else:
                b.sem = self.stack.enter_context(self.nc.semaphore("d_" + b.name))
                b.cnt = 0
                self.nsem += 1
            self.dma_bufs.append(b)
        return b.sem

    def release_sems(self, keep=()):
        rest = []
        for b in self.dma_bufs:
            if any(b is k for k in keep):
                rest.append(b)
            else:
                self.sem_pool.append((b.sem, b.cnt))
                b.sem = None
        self.dma_bufs = rest

    def _wait(self, eng, deps, skip_sem=None):
        w = self.waited[eng]
        for k, (sem, val) in deps.items():
            if skip_sem is not None and sem is skip_sem:
                continue
            if w.get(k, 0) < val:
                self.engs[eng].wait_ge(sem, val)
                w[k] = val
                self.nwait += 1

    def _deps(self, reads, writes, parts, rw):
        deps = {}
        for b in reads:
            _merge(deps, b.writers)
        for b in writes:
            b.new_gen()
            _merge(deps, b.prev)
        for b in parts:
            _merge(deps, b.prev)
        for b in rw:
            _merge(deps, b.prev)
            _merge(deps, b.writers)
            _merge(deps, b.readers)
        return deps

    def _update(self, ev, reads, writes, parts, rw):
        key = id(ev[0])
        for b in reads:
            _merge(b.readers, {key: ev})
        for b in list(writes) + list(parts):
            _merge(b.writers, {key: ev})
        for b in rw:
            b.prev = {}
            b.writers = {key: ev}
            b.readers = {}

    def op(self, eng, fn, reads=(), writes=(), parts=(), rw=()):
        deps = self._deps(reads, writes, parts, rw)
        self._wait(eng, deps, skip_sem=self.esem["pe"] if eng == "pe" else None)
        ins = fn(self.engs[eng])
        self.ecnt[eng] += 1
        ins.then_inc(self.esem[eng], 1)
        ev = (self.esem[eng], self.ecnt[eng])
        self._update(ev, reads, writes, parts, rw)
        return ins

    def dma(self, q, out, in_, sembuf, reads=(), writes=(), parts=(), rw=(), **kw):
        deps = self._deps(reads, writes, parts, rw)
        self._wait(q, deps)
        sem = self._sem_for(sembuf)
        ins = self.engs[q].dma_start(out=out, in_=in_, **kw)
        sembuf.cnt += 16
        ins.then_inc(sem, 16)
        ev = (sem, sembuf.cnt)
        self._update(ev, reads, writes, parts, rw)
        return ins

    def collective(self, kind, rg, in_ap, out_ap, sembuf, reads=(), writes=()):
        deps = self._deps(reads, writes, (), ())
        self._wait("pool", deps)
        sem = self._sem_for(sembuf)
        in_ap = in_ap.opt() if hasattr(in_ap, "opt") else in_ap
        out_ap = out_ap.opt() if hasattr(out_ap, "opt") else out_ap
        ins = self.engs["pool"].collective_compute(kind, ALU.bypass, replica_groups=rg, ins=[in_ap], outs=[out_ap])
        sembuf.cnt += 1
        ins.then_inc(sem)
        ev = (sem, sembuf.cnt)
        self._update(ev, reads, writes, (), ())
        return ins

    def all_events(self):
        deps = {}
        for e in ("pe", "act", "dve", "pool"):
            if self.ecnt[e] > 0:
                deps[id(self.esem[e])] = (self.esem[e], self.ecnt[e])
        for b in self.dma_bufs:
            if b.cnt > 0:
                deps[id(b.sem)] = (b.sem, b.cnt)
        return deps

    def barrier(self, engines=("pe", "act", "dve", "pool", "sp")):
        deps = self.all_events()
        for e in engines:
            self._wait(e, deps)

    def wait_all(self, eng):
        self._wait(eng, self.all_events())


ALPHA = 4 ** 0.25
LN_EPS = 1e-5
EPS_P = LN_EPS / (ALPHA * ALPHA)
NEG = -30000.0
RG = [[0, 1, 2, 3], [4, 5, 6, 7]]


class Cfg:
    def __init__(self, **kw):
        self.D = 2048; self.F = 5632; self.S = 16384; self.TB = 1024
        self.__dict__.update(kw)
        self.T = self.S // 4
        self.TB = min(self.TB, self.T)
        self.KC = self.D // 128
        self.FC = self.F // 128
        self.NJG = self.FC // 4
        self.NQT = self.S // 512
        self.NKT = self.S // 128
        self.rb = {}
        n = 0
        for l in range(2):
            for name, cnt in (("gu1", self.FC), ("d1", (4 * self.NJG) // 2), ("qkv", 24), ("wo", 8),
                              ("gu2", self.FC), ("d2", (4 * self.NJG) // 2)):
                self.rb[(l, name)] = n
                n += cnt
        self.NRB = ((n + 3) // 4) * 4
        self.K = self.NRB // 4


def barrier(kb, exclude=()):
    deps = kb.all_events()
    for b in exclude:
        if b.sem is not None:
            deps.pop(id(b.sem), None)
    for e in ("pe", "act", "dve", "pool", "sp"):
        kb._wait(e, deps)


class Prog:
    def __init__(self, nc, kb, cfg):
        self.nc, self.kb, self.cfg = nc, kb, cfg
        self.uid = 0
        self.AGW = None
        self.rb_off = 0

    def name(self, s):
        self.uid += 1
        return f"{s}_{self.uid}"

    def sb(self, st, n, shape, dt):
        return st.enter_context(self.nc.sbuf_tensor(self.name(n), shape, dt))

    def ps(self, st, n, shape, dt):
        return st.enter_context(self.nc.psum_tensor(self.name(n), shape, dt))

    def bar(self):
        keep = [self.AGW] if self.AGW is not None else []
        barrier(self.kb, exclude=keep)
        self.kb.release_sems(keep=keep)

    def need_rb(self, eng, rb_end):
        cnt = min((rb_end + self.rb_off + 3) // 4, self.AGW.cnt)
        self.kb._wait(eng, {id(self.AGW.sem): (self.AGW.sem, cnt)})


def phase0_weights(P, wshard, shard_bf, blobs):
    nc, kb, cfg = P.nc, P.kb, P.cfg
    P.AGW = kb.buf("AGW")
    with ExitStack() as st:
        f = [P.sb(st, "w0f", [128, 4096], F32) for _ in range(2)]; Fb = kb.bufs(P.name("W0F"), 2)
        b = [P.sb(st, "w0b", [128, 4096], BF16) for _ in range(2)]; Bb = kb.bufs(P.name("W0B"), 2)
        for k in range(cfg.K):
            s = k % 2
            kb.dma("sp", f[s][:], wshard[k], Fb[s], writes=[Fb[s]])
            kb.op("dve", lambda e: e.tensor_copy(out=b[s][:], in_=f[s][:]), reads=[Fb[s]], writes=[Bb[s]])
            SH = kb.buf(P.name("SH"))
            kb.dma("pool", shard_bf[k], b[s][:], Bb[s], reads=[Bb[s]], writes=[SH])
            KL = cfg.K // 2
            kb.collective("AllGather", RG, shard_bf[k], blobs[k // KL][(k % KL) * 512:(k % KL + 1) * 512, :], P.AGW, reads=[SH])
        P.bar()


def ln_sweep(P, st, y_scr, res_src, g_row, b_row, res_dst, xT_dst, r0, ntiles, ident_bf, IDB, plain=False):
    nc, kb, cfg = P.nc, P.kb, P.cfg
    D, KC = cfg.D, cfg.KC
    SD, AD = nc.vector.BN_STATS_DIM, nc.vector.BN_AGGR_DIM
    NCH = D // 512
    x_t = [P.sb(st, "x_t", [128, D], F32) for _ in range(2)]; XT = kb.bufs(P.name("XT"), 2)
    xb = [P.sb(st, "xb", [128, D], BF16) for _ in range(2)]; XB = kb.bufs(P.name("XB"), 2)
    xTo = [P.sb(st, "xTo", [128, KC, 512], BF16) for _ in range(2)]; XTO = kb.bufs(P.name("XTO"), 2)
    pT = [P.ps(st, "pT", [128, D], BF16) for _ in range(2)]; PT = kb.bufs(P.name("PT"), 2)
    if not plain:
        y_t = [P.sb(st, "y_t", [128, D], F32) for _ in range(2)]; YT = kb.bufs(P.name("YT"), 2)
        g_t = P.sb(st, "g_t", [128, D], F32); GT = kb.buf(P.name("GT"))
        b_t = P.sb(st, "b_t", [128, D], F32); BT = kb.buf(P.name("BT"))
        stt = [P.sb(st, "stt", [128, NCH, SD], F32) for _ in range(2)]; STT = kb.bufs(P.name("STT"), 2)
        mv = [P.sb(st, "mv", [128, AD + 2], F32) for _ in range(2)]; MV = kb.bufs(P.name("MV"), 2)
        kb.dma("sp", g_t[:], g_row.partition_broadcast(128), GT, writes=[GT])
        kb.dma("sp", b_t[:], b_row.partition_broadcast(128), BT, writes=[BT])

    def load(i):
        s = i % 2
        r = r0 + i * 128
        if not plain:
            kb.dma("sp", y_t[s][:], y_scr[r:r + 128, :], YT[s], writes=[YT[s]])
        kb.dma("sp", x_t[s][:], res_src[r:r + 128, :], XT[s], writes=[XT[s]])

    load(0)
    for i in range(ntiles):
        s = i % 2
        if i + 1 < ntiles:
            load(i + 1)
        r = r0 + i * 128
        xt = x_t[s]
        if not plain:
            yt = y_t[s]
            kb.op("dve", lambda e: e.tensor_tensor(out=yt[:], in0=yt[:], in1=xt[:], op=ALU.add), reads=[XT[s]], rw=[YT[s]])
            STT[s].new_gen()
            for c in range(NCH):
                kb.op("dve", lambda e, c=c: e.bn_stats(out=stt[s][:, c, :], in_=yt[:, c * 512:(c + 1) * 512]),
                      reads=[YT[s]], parts=[STT[s]])
            m = mv[s]
            kb.op("dve", lambda e: e.bn_aggr(out=m[:, 0:AD], in_=stt[s][:]), reads=[STT[s]], writes=[MV[s]])
            kb.op("act", lambda e: e.activation(out=m[:, AD:AD + 1], in_=m[:, 1:2], func=AF.Ln, bias=EPS_P), rw=[MV[s]])
            kb.op("act", lambda e: e.activation(out=m[:, AD:AD + 1], in_=m[:, AD:AD + 1], func=AF.Exp, scale=-0.5), rw=[MV[s]])
            kb.op("dve", lambda e: e.scalar_tensor_tensor(out=m[:, AD + 1:AD + 2], in0=m[:, 0:1], scalar=-1.0,
                                                          in1=m[:, AD:AD + 1], op0=ALU.mult, op1=ALU.mult), rw=[MV[s]])
            kb.op("act", lambda e: e.activation(out=xt[:], in_=yt[:], func=AF.Identity, bias=m[:, AD + 1:AD + 2],
                                                scale=m[:, AD:AD + 1]), reads=[YT[s], MV[s]], writes=[XT[s]])
            kb.op("pool", lambda e: e.tensor_tensor(out=xt[:], in0=xt[:], in1=g_t[:], op=ALU.mult), reads=[GT], rw=[XT[s]])
            kb.op("pool", lambda e: e.tensor_tensor(out=xt[:], in0=xt[:], in1=b_t[:], op=ALU.add), reads=[BT], rw=[XT[s]])
            kb.dma("pool", res_dst[r:r + 128, :], xt[:], XT[s], reads=[XT[s]])
        if xT_dst is not None:
            kb.op("dve", lambda e: e.tensor_copy(out=xb[s][:], in_=xt[:]), reads=[XT[s]], writes=[XB[s]])
            PT[s].new_gen()
            for c in range(KC):
                kb.op("pe", lambda e, c=c: e.transpose(pT[s][:, c * 128:(c + 1) * 128], xb[s][:, c * 128:(c + 1) * 128],
                                                       ident_bf[:]), reads=[XB[s], IDB], parts=[PT[s]])
            g4, q4 = divmod(i, 4)
            so = g4 % 2
            if q4 == 0:
                XTO[so].new_gen()
            kb.op("act", lambda e: e.activation(out=xTo[so][:, :, q4 * 128:(q4 + 1) * 128],
                                                in_=pT[s][:].rearrange("p (c t) -> p c t", t=128), func=AF.Copy),
                  reads=[PT[s]], parts=[XTO[so]])
            if q4 == 3 or i == ntiles - 1:
                w = (q4 + 1) * 128
                c0 = r0 + g4 * 512
                kb.dma("pool", xT_dst[:, c0:c0 + w].rearrange("(c p) t -> p c t", p=128), xTo[so][:, :, 0:w],
                       XTO[so], reads=[XTO[so]])


def ffn_phase(P, blob, rb_gu, rb_d, g_row, b_row, xT_src, res_src, res_dst, xT_dst, y_scr, ident_bf, IDB):
    nc, kb, cfg = P.nc, P.kb, P.cfg
    D, KC, FC, T, TB, NJG = cfg.D, cfg.KC, cfg.FC, cfg.T, cfg.TB, cfg.NJG
    NPASS, NH, NTT = T // TB, TB // 512, TB // 128
    NCB = D // 512
    yscale = 0.5 / ALPHA
    P.need_rb("sp", rb_d + (NCB * NJG) // 2)
    with ExitStack() as st:
        xT = P.sb(st, "xT", [128, KC, TB], BF16); XTB = kb.buf(P.name("XTB"))
        wgb = [P.sb(st, "wgb", [128, 4096], BF16) for _ in range(3)]; WGB = kb.bufs(P.name("WGB"), 3)
        wdb = [P.sb(st, "wdb", [128, 2048], BF16) for _ in range(3)]; WDB = kb.bufs(P.name("WDB"), 3)
        yb = [P.sb(st, "yb", [128, 512], F32) for _ in range(4)]; YB = kb.bufs(P.name("YB"), 4)
        sg = [P.sb(st, "sg", [128, 512], F32) for _ in range(2)]; SG = kb.bufs(P.name("SG"), 2)
        for ps_i in range(NPASS):
            t0 = ps_i * TB
            with ExitStack() as st2:
                hT = P.sb(st2, "hT", [128, FC * TB], BF16); HT = kb.buf(P.name("HT"))
                acc = [P.ps(st2, "acc", [128, 512], F32) for _ in range(8)]; ACC = kb.bufs(P.name("ACC"), 8)
                kb.dma("sp", xT[:], xT_src[:, t0:t0 + TB].rearrange("(c p) t -> p c t", p=128), XTB, writes=[XTB])
                tasks = [("gu", j) for j in range(FC)] + [("d", cb, jg) for cb in range(NCB) for jg in range(NJG)]

                def load(task):
                    if task[0] == "gu":
                        j = task[1]; s = j % 3
                        r = (rb_gu + j) * 128
                        kb.dma("sp", wgb[s][:], blob[r:r + 128, :], WGB[s], writes=[WGB[s]])
                    else:
                        _, cb, jg = task; tl = cb * NJG + jg; s = tl % 3
                        r = (rb_d + tl // 2) * 128
                        kb.dma("sp", wdb[s][:], blob[r:r + 128, (tl % 2) * 2048:(tl % 2 + 1) * 2048], WDB[s], writes=[WDB[s]])

                load(tasks[0]); load(tasks[1])
                ev_i = 0
                for ti, task in enumerate(tasks):
                    if ti + 2 < len(tasks):
                        load(tasks[ti + 2])
                    if task[0] == "gu":
                        j = task[1]; s = j % 3
                        for hb in range(NH):
                            k = (j * NH + hb) % 2
                            gp, up = acc[2 * k], acc[2 * k + 1]
                            GP, UP = ACC[2 * k], ACC[2 * k + 1]
                            GP.new_gen(); UP.new_gen()
                            for c in range(KC):
                                kb.op("pe", lambda e, c=c: e.matmul(gp[:], lhsT=wgb[s][:, c * 128:(c + 1) * 128],
                                                                    rhs=xT[:, c, hb * 512:(hb + 1) * 512],
                                                                    start=(c == 0), stop=(c == KC - 1)),
                                      reads=[WGB[s], XTB], parts=[GP])
                            for c in range(KC):
                                kb.op("pe", lambda e, c=c: e.matmul(up[:], lhsT=wgb[s][:, (KC + c) * 128:(KC + c + 1) * 128],
                                                                    rhs=xT[:, c, hb * 512:(hb + 1) * 512],
                                                                    start=(c == 0), stop=(c == KC - 1)),
                                      reads=[WGB[s], XTB], parts=[UP])
                            kb.op("act", lambda e: e.activation(out=sg[k][:], in_=gp[:], func=AF.Silu), reads=[GP], writes=[SG[k]])
                            o0 = j * TB + hb * 512
                            kb.op("dve", lambda e: e.tensor_tensor(out=hT[:, o0:o0 + 512], in0=sg[k][:], in1=up[:], op=ALU.mult),
                                  reads=[SG[k], UP], parts=[HT])
                    else:
                        _, cb, jg = task; s = (cb * NJG + jg) % 3
                        for jj in range(4):
                            j = jg * 4 + jj
                            for tt in range(NTT):
                                if j == 0:
                                    ACC[tt].new_gen()
                                kb.op("pe", lambda e, tt=tt, j=j, jj=jj: e.matmul(
                                    acc[tt][:], lhsT=hT[:, j * TB + tt * 128:j * TB + (tt + 1) * 128],
                                    rhs=wdb[s][:, jj * 512:(jj + 1) * 512], start=(j == 0), stop=(j == FC - 1)),
                                    reads=[WDB[s], HT], parts=[ACC[tt]])
                        if jg == NJG - 1:
                            for tt in range(NTT):
                                ys = ev_i % 4; ev_i += 1
                                kb.op("act", lambda e, tt=tt, ys=ys: e.activation(out=yb[ys][:], in_=acc[tt][:], func=AF.Identity, scale=yscale),
                                      reads=[ACC[tt]], writes=[YB[ys]])
                                r = t0 + tt * 128
                                kb.dma("pool", y_scr[r:r + 128, cb * 512:(cb + 1) * 512], yb[ys][:], YB[ys], reads=[YB[ys]])
                P.bar()
            with ExitStack() as st3:
                ln_sweep(P, st3, y_scr, res_src, g_row, b_row, res_dst, xT_dst, t0, NTT, ident_bf, IDB)
                P.bar()


def gather_chunks(P, srcs, dsts):
    kb = P.kb
    AGX = kb.buf(P.name("AGX"))
    for s_ap, d_ap in zip(srcs, dsts):
        kb.collective("AllGather", RG, s_ap, d_ap, AGX)
    P.bar()


def qkv_phase(P, blob, rb_qkv, xg_dst, qT_scr, kT_scr, v_scr, fox, wf_own, bf_own, lf_scr, pid_li):
    nc, kb, cfg = P.nc, P.kb, P.cfg
    KC, S, T = cfg.KC, cfg.S, cfg.T
    NB = S // 512
    qscale = 128 ** -0.5
    P.need_rb("pool", rb_qkv + 24)
    with ExitStack() as st:
        w = P.sb(st, "wqkv", [128, KC * 1536], BF16); W = kb.buf(P.name("WQKV"))
        WOWN = kb.buf(P.name("WOWN"))
        pid_li = nc.gpsimd.partition_id() % 4
        kb.dma("pool", P.wown[:, :], blob[bass.ds(pid_li * 768 + rb_qkv * 128, 768), :], WOWN, writes=[WOWN])
        for m in range(6):
            kb.dma("pool", w[:, m * 4096:(m + 1) * 4096], P.wown[m * 128:(m + 1) * 128, :], W, reads=[WOWN], parts=[W])
        xT = [P.sb(st, "qx", [128, KC, 512], BF16) for _ in range(2)]; XTB = kb.bufs(P.name("QX"), 2)
        qk_o = [P.sb(st, "qko", [128, 8, 512], BF16) for _ in range(2)]; QKO = kb.bufs(P.name("QKO"), 2)
        v_o = [P.sb(st, "vo", [128, 4, 512], BF16) for _ in range(2)]; VO = kb.bufs(P.name("VO"), 2)
        acc = [P.ps(st, "qacc", [128, 512], F32) for _ in range(4)]; ACC = kb.bufs(P.name("QACC"), 4)
        if fox:
            wff = P.sb(st, "wff", [128, KC * 4], F32); WFF = kb.buf(P.name("WFF"))
            wfb = P.sb(st, "wfb", [128, KC * 4], BF16); WFB = kb.buf(P.name("WFB"))
            bfc = P.sb(st, "bfc", [4, 2], F32); BFC = kb.buf(P.name("BFC"))
            kb.dma("sp", wff[:], wf_own[:, :], WFF, writes=[WFF])
            kb.dma("sp", bfc[:, 0:1], bf_own[:, :], BFC, writes=[BFC])
            kb.op("dve", lambda e: e.tensor_copy(out=wfb[:], in_=wff[:]), reads=[WFF], writes=[WFB])
            kb.op("dve", lambda e: e.tensor_scalar(out=bfc[:, 1:2], in0=bfc[:, 0:1], scalar1=-1.0, scalar2=None, op0=ALU.mult), rw=[BFC])
            facc = P.ps(st, "facc", [4, 512], F32); FACC = kb.buf(P.name("FACC"))
            lf = [P.sb(st, "lf", [4, 512], F32) for _ in range(2)]; LF = kb.bufs(P.name("LF"), 2)

        def load(b):
            s = b % 2
            r, t0 = divmod(b * 512, T)
            kb.dma("sp", xT[s][:], xg_dst[:, r * 128:(r + 1) * 128, t0:t0 + 512].rearrange("c p t -> p c t"), XTB[s],
                   writes=[XTB[s]])

        load(0)
        ai = 0
        for b in range(NB):
            s = b % 2
            if b + 1 < NB:
                load(b + 1)
            x = xT[s]
            QKO[s].new_gen()
            for o in range(8):
                a = ai % 4; ai += 1
                ACC[a].new_gen()
                for c in range(KC):
                    kb.op("pe", lambda e, c=c: e.matmul(acc[a][:], lhsT=w[:, c * 1536 + o * 128:c * 1536 + (o + 1) * 128],
                                                        rhs=x[:, c, :], start=(c == 0), stop=(c == KC - 1)),
                          reads=[W, XTB[s]], parts=[ACC[a]])
                eng = "act" if o % 2 == 0 else "dve"
                sc = qscale if o < 4 else 1.0
                if eng == "act":
                    kb.op("act", lambda e: e.activation(out=qk_o[s][:, o, :], in_=acc[a][:], func=AF.Identity, scale=sc),
                          reads=[ACC[a]], parts=[QKO[s]])
                else:
                    kb.op("dve", lambda e: e.tensor_scalar(out=qk_o[s][:, o, :], in0=acc[a][:], scalar1=sc, scalar2=None, op0=ALU.mult),
                          reads=[ACC[a]], parts=[QKO[s]])
            kb.dma("pool", qT_scr[:, :, b * 512:(b + 1) * 512].rearrange("u p t -> p u t"), qk_o[s][:, 0:4, :], QKO[s], reads=[QKO[s]])
            kb.dma("pool", kT_scr[:, :, b * 512:(b + 1) * 512].rearrange("u p t -> p u t"), qk_o[s][:, 4:8, :], QKO[s], reads=[QKO[s]])
            VO[s].new_gen()
            for tt in range(4):
                a = ai % 4; ai += 1
                ACC[a].new_gen()
                for c in range(KC):
                    kb.op("pe", lambda e, c=c: e.matmul(acc[a][:], lhsT=x[:, c, tt * 128:(tt + 1) * 128],
                                                        rhs=w[:, c * 1536 + 1024:c * 1536 + 1536], start=(c == 0), stop=(c == KC - 1)),
                          reads=[W, XTB[s]], parts=[ACC[a]])
                if tt % 2 == 0:
                    kb.op("act", lambda e: e.activation(out=v_o[s][:, tt, :], in_=acc[a][:], func=AF.Copy), reads=[ACC[a]], parts=[VO[s]])
                else:
                    kb.op("dve", lambda e: e.tensor_copy(out=v_o[s][:, tt, :], in_=acc[a][:]), reads=[ACC[a]], parts=[VO[s]])
            kb.dma("pool", v_scr[b * 512:(b + 1) * 512, :].rearrange("(tt p) e -> p tt e", p=128), v_o[s][:], VO[s], reads=[VO[s]])
            if fox:
                FACC.new_gen()
                for c in range(KC):
                    kb.op("pe", lambda e, c=c: e.matmul(facc[:], lhsT=wfb[:, c * 4:(c + 1) * 4], rhs=x[:, c, :],
                                                        start=(c == 0), stop=(c == KC - 1)), reads=[WFB, XTB[s]], parts=[FACC])
                kb.op("act", lambda e: e.activation(out=lf[s][:], in_=facc[:], func=AF.Exp, bias=bfc[:, 1:2], scale=-1.0),
                      reads=[FACC, BFC], writes=[LF[s]])
                kb.op("act", lambda e: e.activation(out=lf[s][:], in_=lf[s][:], func=AF.Ln, bias=1.0), rw=[LF[s]])
                kb.dma("pool", lf_scr[:, b * 512:(b + 1) * 512], lf[s][:], LF[s], reads=[LF[s]])
        P.bar()


def fox_cumsum(P, lf_scr, negc_scr, mt_dram):
    nc, kb, cfg = P.nc, P.kb, P.cfg
    SEG = cfg.S // 32
    with ExitStack() as st:
        L = P.sb(st, "cL", [128, SEG], F32); LB = kb.buf(P.name("CL"))
        Z = P.sb(st, "cZ", [128, SEG], F32); ZB = kb.buf(P.name("CZ"))
        C = P.sb(st, "cC", [128, SEG], F32); CB = kb.buf(P.name("CC"))
        MT = P.sb(st, "cMT", [128, 128], F32); MTB = kb.buf(P.name("CMT"))
        off = P.sb(st, "coff", [128, 1], F32); OFF = kb.buf(P.name("COFF"))
        pp = P.ps(st, "cpp", [128, 2], F32); PP = kb.buf(P.name("CPP"))
        kb.dma("sp", L[:], lf_scr.rearrange("u (g s) -> (u g) s", s=SEG), LB, writes=[LB])
        kb.dma("sp", MT[:], mt_dram[:, :], MTB, writes=[MTB])
        kb.op("dve", lambda e: e.memset(Z[:], 0.0), writes=[ZB])
        kb.op("dve", lambda e: e.tensor_tensor_scan(out=C[:], data0=Z[:], data1=L[:], initial=0.0, op0=ALU.add, op1=ALU.add),
              reads=[ZB, LB], writes=[CB])
        kb.op("pe", lambda e: e.matmul(pp[:, 0:1], lhsT=MT[:, :], rhs=C[:, SEG - 1:SEG], start=True, stop=True),
              reads=[MTB, CB], writes=[PP])
        kb.op("dve", lambda e: e.tensor_copy(out=off[:], in_=pp[:, 0:1]), reads=[PP], writes=[OFF])
        kb.op("dve", lambda e: e.tensor_scalar(out=C[:], in0=C[:], scalar1=off[:, 0:1], scalar2=None, op0=ALU.add), reads=[OFF], rw=[CB])
        kb.dma("pool", negc_scr.rearrange("u (g s) -> (u g) s", s=SEG), C[:], CB, reads=[CB])
        P.bar()


def attention_phase(P, fox, qT_scr, kT_scr, v_scr, oT_src, ident_bf, IDB, ident_f, IDF, tri_dram, negc_scr,
                    relm, lqk, subg, lam_init):
    nc, kb, cfg = P.nc, P.kb, P.cfg
    S, T, NQT, NKT = cfg.S, cfg.T, cfg.NQT, cfg.NKT
    VW = 128 if fox else 256
    NHEAD = 4 if fox else 2
    with ExitStack() as st:
        KTs = [P.sb(st, "aKT", [128, S], BF16) for _ in range(1 if fox else 2)]; KTBs = kb.bufs(P.name("AKT"), 1 if fox else 2)
        V = P.sb(st, "aV", [128, NKT, VW + 1], BF16); VB = kb.buf(P.name("AV"))
        QT = [P.sb(st, "aQT", [128, 512], BF16) for _ in range(2)]; QTB = kb.bufs(P.name("AQT"), 2)
        PTl = [P.sb(st, "aP", [128, 512], BF16) for _ in range(4)]; PB = kb.bufs(P.name("AP"), 4)
        TMP = [P.sb(st, "aTMP", [128, 512], F32) for _ in range(2)]; TMPB = kb.bufs(P.name("ATMP"), 2)
        sps = [P.ps(st, "aS", [128, 512], F32) for _ in range(2)]; SPS = kb.bufs(P.name("AS"), 2)
        ops_ = [P.ps(st, "aO", [128, 512], F32) for _ in range(4)]; OPS = kb.bufs(P.name("AO"), 4)
        tps = P.ps(st, "aT", [128, 1024], BF16); TPS = kb.buf(P.name("AT"))
        osb = [P.sb(st, "aosb", [128, VW + 1], F32) for _ in range(2)]; OSB = kb.bufs(P.name("AOSB"), 2)
        rl = [P.sb(st, "arl", [128, 4], F32) for _ in range(2)]; RL = kb.bufs(P.name("ARL"), 2)
        onb = [P.sb(st, "aonb", [128, VW], BF16) for _ in range(2)]; ONB = kb.bufs(P.name("AONB"), 2)
        oT = [P.sb(st, "aoT", [128, VW // 128, 512], BF16) for _ in range(2)]; OTB = kb.bufs(P.name("AOT"), 2)
        if fox:
            tri_f = P.sb(st, "atrif", [128, 128], F32); TRIF = kb.buf(P.name("ATRIF"))
            tri = P.sb(st, "atri", [128, 128], BF16); TRI = kb.buf(P.name("ATRI"))
            kb.dma("sp", tri_f[:], tri_dram[:, :], TRIF, writes=[TRIF])
            kb.op("dve", lambda e: e.tensor_copy(out=tri[:], in_=tri_f[:]), reads=[TRIF], writes=[TRI])
            ncr = P.sb(st, "ancr", [NKT, 128], F32); NCR = kb.buf(P.name("ANCR"))
            negck = P.sb(st, "anegck", [128, NKT], F32); NEGCK = kb.buf(P.name("ANEGCK"))
            cbc = [P.sb(st, "acbc", [128, 512], F32) for _ in range(2)]; CBC = kb.bufs(P.name("ACBC"), 2)
            ncp = P.ps(st, "ancp", [128, NKT], F32); NCP = kb.buf(P.name("ANCP"))
        else:
            M = P.sb(st, "aM", [128, 1024], F32); MB = kb.buf(P.name("AM"))
            o1n = P.sb(st, "ao1n", [128, 4, 256], F32); O1N = kb.bufs(P.name("AO1N"), 4)
            od = [P.sb(st, "aod", [128, 256], F32) for _ in range(2)]; OD = kb.bufs(P.name("AOD"), 2)
            sqj = P.sb(st, "asqj", [128, 256], F32); SQJ = kb.buf(P.name("ASQJ"))
            lq = P.sb(st, "alq", [128, 4, 128], F32); LQ = kb.buf(P.name("ALQ"))
            lam = P.sb(st, "alam", [128, 8], F32); LAM = kb.buf(P.name("ALAM"))
            gsb = P.sb(st, "agsb", [128, 256], F32); GSB = kb.buf(P.name("AGSB"))
            for i in range(4):
                kb.dma("sp", lq[:, i, :], lqk[i:i + 1, :].partition_broadcast(128), LQ, parts=[LQ])
            kb.dma("sp", gsb[:], subg[0:1, :].partition_broadcast(128), GSB, writes=[GSB])
            kb.op("dve", lambda e: e.tensor_tensor(out=lq[:, 0, :], in0=lq[:, 0, :], in1=lq[:, 1, :], op=ALU.mult), rw=[LQ])
            kb.op("dve", lambda e: e.tensor_tensor(out=lq[:, 2, :], in0=lq[:, 2, :], in1=lq[:, 3, :], op=ALU.mult), rw=[LQ])
            kb.op("dve", lambda e: e.reduce_sum(out=lam[:, 0:1], in_=lq[:, 0, :], axis=AX.X), reads=[LQ], writes=[LAM])
            kb.op("dve", lambda e: e.reduce_sum(out=lam[:, 1:2], in_=lq[:, 2, :], axis=AX.X), reads=[LQ], rw=[LAM])
            kb.op("act", lambda e: e.activation(out=lam[:, 2:4], in_=lam[:, 0:2], func=AF.Exp), rw=[LAM])
            kb.op("dve", lambda e: e.tensor_tensor(out=lam[:, 4:5], in0=lam[:, 3:4], in1=lam[:, 2:3], op=ALU.subtract), rw=[LAM])
            kb.op("dve", lambda e: e.tensor_scalar(out=lam[:, 5:6], in0=lam[:, 4:5], scalar1=-lam_init, scalar2=None, op0=ALU.add), rw=[LAM])
            kb.op("dve", lambda e: e.tensor_scalar(out=gsb[:], in0=gsb[:], scalar1=1.0 - lam_init, scalar2=None, op0=ALU.mult), rw=[GSB])
            neglam = lam[:, 5:6]
            ssum = [P.sb(st, "assum", [128, 2], F32) for _ in range(2)]; SSUM = kb.bufs(P.name("ASSUM"), 2)

        pi = 0
        si = 0
        fi = 0
        NM = 1 if fox else 2
        for head in range(NHEAD):
            for mp_ in range(NM):
                kb.dma("sp", KTs[mp_][:], kT_scr[head * NM + mp_], KTBs[mp_], writes=[KTBs[mp_]])
            VB.new_gen()
            vsrc = v_scr[:, head * VW:(head + 1) * VW].rearrange("(kt p) e -> p kt e", p=128)
            nvs = max(1, NKT // 32)
            for vi in range(nvs):
                k0, k1 = vi * (NKT // nvs), (vi + 1) * (NKT // nvs)
                kb.dma("sp", V[:, k0:k1, 0:VW], vsrc[:, k0:k1, :], VB, parts=[VB])
            kb.op("pool", lambda e: e.memset(V[:, :, VW:VW + 1], 1.0), parts=[VB])
            if fox:
                kb.dma("sp", ncr[:], negc_scr[head:head + 1, :].rearrange("o (kt p) -> (o kt) p", p=128), NCR, writes=[NCR])
                kb.op("pe", lambda e: e.transpose(ncp[:], ncr[:], ident_f[0:NKT, 0:NKT]), reads=[NCR, IDF], writes=[NCP])
                kb.op("dve", lambda e: e.tensor_copy(out=negck[:], in_=ncp[:]), reads=[NCP], writes=[NEGCK])
            else:
                kb.dma("sp", M[:], relm[head], MB, writes=[MB])
            units = [(t, mp) for t in range(NQT) for mp in range(NM)]
            steps = [(ui, kt) for ui, (t, mp) in enumerate(units) for kt in range(4 * t + 4)]
            score_slot = {}

            def emit_qload(ui):
                t, mp = units[ui]
                u = head * NM + mp
                qs = ui % 2
                kb.dma("sp", QT[qs][:], qT_scr[u, :, t * 512:(t + 1) * 512], QTB[qs], writes=[QTB[qs]])
                if fox:
                    kb.dma("sp", cbc[qs][:], negc_scr[u:u + 1, t * 512:(t + 1) * 512].partition_broadcast(128), CBC[qs], writes=[CBC[qs]])

            def emit_score(sidx):
                nonlocal si
                ui, kt = steps[sidx]
                t, mp = units[ui]
                c0 = max(kt - 4 * t, 0) * 128
                qs = ui % 2
                sp = si % 2; si += 1
                kb.op("pe", lambda e: e.matmul(sps[sp][:, c0:512], lhsT=KTs[mp][:, kt * 128:(kt + 1) * 128], rhs=QT[qs][:, c0:512],
                                               start=True, stop=True), reads=[KTBs[mp], QTB[qs]], writes=[SPS[sp]])
                score_slot[sidx] = sp

            emit_qload(0)
            emit_score(0)
            sidx = 0
            for ui, (t, mp) in enumerate(units):
                u = head * NM + mp
                qs_ = ui % 2
                if ui + 1 < len(units):
                    emit_qload(ui + 1)
                for q in range(4):
                    OPS[q].new_gen()
                nk = 4 * t + 4
                for kt in range(nk):
                    j = kt - 4 * t
                    c0 = max(j, 0) * 128
                    if sidx + 1 < len(steps):
                        emit_score(sidx + 1)
                    sp_ = score_slot.pop(sidx); sidx += 1
                    pp_ = pi % 4; pi += 1
                    if fox:
                        tm = TMP[sp_]
                        kb.op("dve", lambda e: e.tensor_tensor(out=tm[:, c0:512], in0=sps[sp_][:, c0:512], in1=cbc[qs_][:, c0:512],
                                                               op=ALU.subtract), reads=[SPS[sp_], CBC[qs_]], writes=[TMPB[sp_]])
                        kb.op("act", lambda e: e.activation(out=PTl[pp_][:, c0:512], in_=tm[:, c0:512], func=AF.Exp,
                                                            bias=negck[:, kt:kt + 1]), reads=[TMPB[sp_], NEGCK], writes=[PB[pp_]])
                        if j >= 0:
                            kb.op("pool", lambda e: e.tensor_tensor(out=PTl[pp_][:, c0:c0 + 128], in0=PTl[pp_][:, c0:c0 + 128],
                                                                    in1=tri[:], op=ALU.mult), reads=[TRI], rw=[PB[pp_]])
                    else:
                        if j >= -1:
                            tm = TMP[sp_]
                            x0 = 384 - j * 128 + c0
                            kb.op("dve", lambda e: e.tensor_tensor(out=tm[:, c0:512], in0=sps[sp_][:, c0:512], in1=M[:, x0:x0 + 512 - c0],
                                                                   op=ALU.add), reads=[SPS[sp_], MB], writes=[TMPB[sp_]])
                            kb.op("act", lambda e: e.activation(out=PTl[pp_][:, c0:512], in_=tm[:, c0:512], func=AF.Exp),
                                  reads=[TMPB[sp_]], writes=[PB[pp_]])
                        else:
                            kb.op("act", lambda e: e.activation(out=PTl[pp_][:, :], in_=sps[sp_][:, :], func=AF.Exp,
                                                                bias=M[:, 1023:1024]), reads=[SPS[sp_], MB], writes=[PB[pp_]])
                    for q in range(max(j, 0), 4):
                        last = (kt == 4 * t + q)
                        kb.op("pe", lambda e, q=q: e.matmul(ops_[q][:, 0:VW + 1], lhsT=PTl[pp_][:, q * 128:(q + 1) * 128], rhs=V[:, kt, :],
                                                            start=(kt == 0), stop=last), reads=[PB[pp_], VB], parts=[OPS[q]])
                ot = fi % 2
                if fox or mp == 1:
                    OTB[ot].new_gen()
                for q in range(4):
                    f2 = (fi * 4 + q) % 2
                    ob, r_ = osb[f2], rl[f2]
                    kb.op("act", lambda e: e.activation(out=ob[:], in_=ops_[q][:, 0:VW + 1], func=AF.Copy), reads=[OPS[q]], writes=[OSB[f2]])
                    kb.op("dve", lambda e: e.reciprocal(out=r_[:, 0:1], in_=ob[:, VW:VW + 1]), reads=[OSB[f2]], writes=[RL[f2]])
                    if fox:
                        kb.op("dve", lambda e: e.tensor_scalar(out=onb[f2][:], in0=ob[:, 0:VW], scalar1=r_[:, 0:1], scalar2=None, op0=ALU.mult),
                              reads=[OSB[f2], RL[f2]], writes=[ONB[f2]])
                    elif mp == 0:
                        kb.op("dve", lambda e: e.tensor_scalar(out=o1n[:, q, :], in0=ob[:, 0:VW], scalar1=r_[:, 0:1], scalar2=None, op0=ALU.mult),
                              reads=[OSB[f2], RL[f2]], writes=[O1N[q]])
                        continue
                    else:
                        d_ = od[f2]
                        kb.op("dve", lambda e: e.tensor_scalar(out=r_[:, 1:2], in0=r_[:, 0:1], scalar1=neglam, scalar2=None, op0=ALU.mult),
                              reads=[LAM], rw=[RL[f2]])
                        kb.op("dve", lambda e: e.scalar_tensor_tensor(out=d_[:], in0=ob[:, 0:VW], scalar=r_[:, 1:2], in1=o1n[:, q, :],
                                                                      op0=ALU.mult, op1=ALU.add), reads=[OSB[f2], RL[f2], O1N[q]], writes=[OD[f2]])
                        ss_ = ssum[f2]
                        kb.op("act", lambda e: e.activation(out=sqj[:], in_=d_[:], func=AF.Square, accum_out=ss_[:, 0:1]),
                              reads=[OD[f2]], writes=[SQJ, SSUM[f2]])
                        kb.op("act", lambda e: e.activation(out=ss_[:, 1:2], in_=ss_[:, 0:1], func=AF.Ln, bias=LN_EPS, scale=1.0 / 256), rw=[SSUM[f2]])
                        kb.op("act", lambda e: e.activation(out=ss_[:, 1:2], in_=ss_[:, 1:2], func=AF.Exp, scale=-0.5), rw=[SSUM[f2]])
                        kb.op("dve", lambda e: e.scalar_tensor_tensor(out=onb[f2][:], in0=d_[:], scalar=ss_[:, 1:2], in1=gsb[:],
                                                                      op0=ALU.mult, op1=ALU.mult), reads=[OD[f2], SSUM[f2], GSB], writes=[ONB[f2]])
                    TPS.new_gen()
                    for c in range(VW // 128):
                        kb.op("pe", lambda e, c=c: e.transpose(tps[:, c * 128:(c + 1) * 128], onb[f2][:, c * 128:(c + 1) * 128], ident_bf[:]),
                              reads=[ONB[f2], IDB], parts=[TPS])
                    kb.op("act", lambda e: e.activation(out=oT[ot][:, :, q * 128:(q + 1) * 128],
                                                        in_=tps[:, 0:VW].rearrange("p (c t) -> p c t", t=128), func=AF.Copy),
                          reads=[TPS], parts=[OTB[ot]])
                if fox or mp == 1:
                    tc, tq = divmod(t * 512, T)
                    for c in range(VW // 128):
                        fc = u if fox else head * 2 + c
                        kb.dma("pool", oT_src[fc, tc, :, tq:tq + 512], oT[ot][:, c, :], OTB[ot], reads=[OTB[ot]])
                fi += 1
        P.bar()


def wo_phase(P, blob, rb_wo, oT_dst2, y_scr, pid_li):
    nc, kb, cfg = P.nc, P.kb, P.cfg
    KC, T, D = cfg.KC, cfg.T, cfg.D
    P.need_rb("sp", rb_wo + 8)
    with ExitStack() as st:
        w = P.sb(st, "wo", [128, KC * D], BF16); W = kb.buf(P.name("WO"))
        for m in range(8):
            r = (rb_wo + m) * 128
            kb.dma("sp", w[:, m * 4096:(m + 1) * 4096], blob[r:r + 128, :], W, parts=[W])
        oT = [P.sb(st, "woT", [128, KC, 512], BF16) for _ in range(2)]; OT = kb.bufs(P.name("WOT"), 2)
        acc = [P.ps(st, "wacc", [128, 512], F32) for _ in range(4)]; ACC = kb.bufs(P.name("WACC"), 4)
        yb = [P.sb(st, "wyb", [128, 512], F32) for _ in range(4)]; YB = kb.bufs(P.name("WYB"), 4)
        NG = T // 512

        OMINE = kb.buf(P.name("OMINE"))
        pid_sp = nc.sync.partition_id() % 4
        base = oT_dst2[bass.ds(pid_sp * 2048, 2048), :]
        for fc in range(4):
            kb.dma("sp", P.omine[fc], base[fc * 512:(fc + 1) * 512, :], OMINE, parts=[OMINE])

        def load(g):
            s = g % 2
            OT[s].new_gen()
            for fc in range(4):
                for r in range(4):
                    ci = r * 4 + fc
                    kb.dma("sp", oT[s][:, ci, :], P.omine[fc, r * 128:(r + 1) * 128, g * 512:(g + 1) * 512], OT[s],
                           reads=[OMINE], parts=[OT[s]])

        load(0)
        ai = 0
        for g in range(NG):
            s = g % 2
            if g + 1 < NG:
                load(g + 1)
            for tt in range(4):
                for cb in range(4):
                    a = ai % 4; ai += 1
                    ACC[a].new_gen()
                    for c in range(KC):
                        kb.op("pe", lambda e, c=c: e.matmul(acc[a][:], lhsT=oT[s][:, c, tt * 128:(tt + 1) * 128],
                                                            rhs=w[:, c * D + cb * 512:c * D + (cb + 1) * 512], start=(c == 0), stop=(c == KC - 1)),
                              reads=[W, OT[s]], parts=[ACC[a]])
                    kb.op("act", lambda e: e.activation(out=yb[a][:], in_=acc[a][:], func=AF.Identity, scale=1.0 / ALPHA),
                          reads=[ACC[a]], writes=[YB[a]])
                    r0 = g * 512 + tt * 128
                    kb.dma("pool", y_scr[r0:r0 + 128, cb * 512:(cb + 1) * 512], yb[a][:], YB[a], reads=[YB[a]])
        P.bar()


def build_program(cfg):
    nc = bass.Bass("TRN2", target_bir_lowering=False)
    D, T, S, K = cfg.D, cfg.T, cfg.S, cfg.K
    dt_in = lambda n, s: nc.dram_tensor(n, s, F32, kind="ExternalInput").ap()
    x_tm = dt_in("x_tm", [T, D])
    wshard = dt_in("wshard", [K, 128, 4096])
    gb = dt_in("gb", [12, D])
    relm = dt_in("relm", [2, 128, 1024])
    lqk = dt_in("lqk", [4, 128])
    subg = dt_in("subg", [1, 256])
    wf_own = dt_in("wf_own", [128, cfg.KC * 4])
    bf_own = dt_in("bf_own", [4, 1])
    ident = dt_in("ident", [128, 128])
    tri = dt_in("tri", [128, 128])
    mt = dt_in("mt", [128, 128])
    out = nc.dram_tensor("out", [T, D], F32, kind="ExternalOutput").ap()
    di = lambda n, s, d: nc.dram_tensor(n, s, d)
    RBL = cfg.NRB // 2
    assert RBL % 4 == 0 and cfg.NRB == 2 * cfg.rb[(1, "gu1")]
    blobs = [di(f"blob{l}", [RBL * 128, 4096], BF16) for l in range(2)]
    shard_bf = di("shard_bf", [K, 128, 4096], BF16)
    xT_A = di("xT_A", [D, T], BF16)
    xT_B = di("xT_B", [D, T], BF16)
    xg_src = di("xg_src", [D, T], BF16)
    xg_dst = di("xg_dst", [cfg.KC, 512, T], BF16)
    res = [di(f"res{i}", [T, D], F32) for i in range(3)]
    y_scr = di("y_scr", [T, D], F32)
    qT_scr = di("qT_scr", [4, 128, S], BF16)
    kT_scr = di("kT_scr", [4, 128, S], BF16)
    v_scr = di("v_scr", [S, 512], BF16)
    lf_scr = di("lf_scr", [4, S], F32)
    negc_scr = di("negc_scr", [4, S], F32)
    oT_src = di("oT_src", [4, 4, 128, T], BF16)
    oT_dst = di("oT_dst", [4, 4, 512, T], BF16)
    oT_dst2 = oT_dst.reshape([4 * 4 * 512, T])
    wown = di("wown", [768, 4096], BF16)
    omine = di("omine", [4, 512, T], BF16)
    with ExitStack() as st:
        kb = KB(nc, st)
        P = Prog(nc, kb, cfg)
        P.wown, P.omine = wown, omine
        st.enter_context(nc.Block())
        pid_li = nc.gpsimd.partition_id() % 4
        idf = P.sb(st, "idf", [128, 128], F32); IDF = kb.buf("IDF")
        idb = P.sb(st, "idb", [128, 128], BF16); IDB = kb.buf("IDB")
        kb.dma("sp", idf[:], ident[:, :], IDF, writes=[IDF])
        kb.op("dve", lambda e: e.tensor_copy(out=idb[:], in_=idf[:]), reads=[IDF], writes=[IDB])
        phase0_weights(P, wshard, shard_bf, blobs)
        with ExitStack() as st1:
            ln_sweep(P, st1, None, x_tm, None, None, None, xT_A, 0, T // 128, idb, IDB, plain=True)
            P.bar()
        r_in = x_tm
        for l in range(2):
            fox = (l == 1)
            rb = lambda n: cfg.rb[(l, n)] - l * RBL
            blob = blobs[l]
            P.rb_off = l * RBL
            grow = lambda i: (gb[(l * 3 + i) * 2:(l * 3 + i) * 2 + 1, :], gb[(l * 3 + i) * 2 + 1:(l * 3 + i) * 2 + 2, :])
            g_, b_ = grow(0)
            ffn_phase(P, blob, rb("gu1"), rb("d1"), g_, b_, xT_A, r_in, res[0], xg_src, y_scr, idb, IDB)
            gather_chunks(P, [xg_src[c * 128:(c + 1) * 128, :] for c in range(cfg.KC)], [xg_dst[c] for c in range(cfg.KC)])
            qkv_phase(P, blob, rb("qkv"), xg_dst, qT_scr, kT_scr, v_scr, fox, wf_own, bf_own, lf_scr, pid_li)
            if fox:
                fox_cumsum(P, lf_scr, negc_scr, mt)
            attention_phase(P, fox, qT_scr, kT_scr, v_scr, oT_src, idb, IDB, idf, IDF, tri, negc_scr, relm, lqk, subg,
                            0.8 - 0.6 * math.exp(-0.3 * l))
            gather_chunks(P, [oT_src[fc, tc] for tc in range(4) for fc in range(4)],
                          [oT_dst[tc, fc] for tc in range(4) for fc in range(4)])
            wo_phase(P, blob, rb("wo"), oT_dst2, y_scr, pid_li)
            g_, b_ = grow(1)
            with ExitStack() as st2:
                ln_sweep(P, st2, y_scr, res[0], g_, b_, res[1], xT_B, 0, T // 128, idb, IDB)
                P.bar()
            g_, b_ = grow(2)
            last = (l == 1)
            ffn_phase(P, blob, rb("gu2"), rb("d2"), g_, b_, xT_B, res[1], out if last else res[2], None if last else xT_A,
                      y_scr, idb, IDB)
            r_in = res[2]
        kb.wait_all("pool")
        kb.wait_all("sp")
        build_program.stats = dict(sems=kb.nsem, waits=kb.nwait, counts=dict(kb.ecnt))
    return nc


def _t5_bucket_np(n):
    n = np.maximum(n, 0)
    nf = np.maximum(n, 1).astype(np.float32)
    large = 16 + (np.log(nf / np.float32(16)) / np.float32(math.log(128 / 16)) * np.float32(16)).astype(np.int32)
    large = np.minimum(large, 31)
    return np.where(n < 16, n, large)


def _tile_gu(wg, wu, KC, FC):
    out = np.empty((FC, 128, 2, KC, 128), np.float32)
    for m, w in enumerate((wg, wu)):
        out[:, :, m] = w.reshape(KC, 128, FC, 128).transpose(2, 1, 0, 3)
    return out.reshape(FC, 128, 4096)


def _tile_d(wd, FC):
    t = wd.reshape(FC // 4, 4, 128, 4, 512).transpose(3, 0, 2, 1, 4).reshape(FC, 128, 2048)
    return np.ascontiguousarray(t.reshape(FC // 2, 2, 128, 2048).transpose(0, 2, 1, 3)).reshape(FC // 2, 128, 4096)


def _tile_rows(w, KC):
    N = w.shape[1]
    t = w.reshape(KC, 128, N).transpose(1, 0, 2).reshape(128, KC * N)
    return np.ascontiguousarray(t.reshape(128, KC * N // 4096, 4096).transpose(1, 0, 2))


def pack_inputs(cfg, inp):
    D, S, T, KC, FC = cfg.D, cfg.S, cfg.T, cfg.KC, cfg.FC
    f32 = lambda a: np.asarray(a, dtype=np.float32)
    rbs = np.zeros((cfg.NRB, 128, 4096), np.float32)
    for l in range(2):
        rbs[cfg.rb[(l, "gu1")]:cfg.rb[(l, "gu1")] + FC] = _tile_gu(f32(inp["ffn1_wg"][l]), f32(inp["ffn1_wu"][l]), KC, FC)
        rbs[cfg.rb[(l, "d1")]:cfg.rb[(l, "d1")] + FC // 2] = _tile_d(f32(inp["ffn1_wd"][l]), FC)
        rbs[cfg.rb[(l, "gu2")]:cfg.rb[(l, "gu2")] + FC] = _tile_gu(f32(inp["ffn2_wg"][l]), f32(inp["ffn2_wu"][l]), KC, FC)
        rbs[cfg.rb[(l, "d2")]:cfg.rb[(l, "d2")] + FC // 2] = _tile_d(f32(inp["ffn2_wd"][l]), FC)
        wqkv = f32(inp["diff_wqkv"][0] if l == 0 else inp["fox_wqkv"][0])
        wo = f32(inp["diff_wo"][0] if l == 0 else inp["fox_wo"][0])
        for r in range(4):
            own = np.concatenate([wqkv[:, r * 512:(r + 1) * 512], wqkv[:, D + r * 512:D + (r + 1) * 512],
                                  wqkv[:, 2 * D + r * 512:2 * D + (r + 1) * 512]], axis=1)
            b0 = cfg.rb[(l, "qkv")] + r * 6
            rbs[b0:b0 + 6] = _tile_rows(own, KC)
        rbs[cfg.rb[(l, "wo")]:cfg.rb[(l, "wo")] + 8] = _tile_rows(wo, KC)
    shards = [np.ascontiguousarray(rbs[r::4]) for r in range(4)]
    gb = np.empty((12, D), np.float32)
    for l in range(2):
        for i in range(3):
            gb[(l * 3 + i) * 2] = inp["ln_g"][l, i]
            gb[(l * 3 + i) * 2 + 1] = inp["ln_b"][l, i]
    kp = np.arange(128)[:, None]; xx = np.arange(1024)[None, :]
    n = xx - kp - 384
    bucket = _t5_bucket_np(n)
    rel = f32(inp["rel_table"])
    ident = np.eye(128, dtype=np.float32)
    tri = (np.arange(128)[None, :] >= np.arange(128)[:, None]).astype(np.float32)
    pp = np.arange(128)
    mt = ((pp[:, None] // 32 == pp[None, :] // 32) & (pp[:, None] < pp[None, :])).astype(np.float32)
    lqk = np.stack([f32(inp["diff_lq1"][0]), f32(inp["diff_lk1"][0]), f32(inp["diff_lq2"][0]), f32(inp["diff_lk2"][0])])
    subg = f32(inp["diff_subln_g"][0]).reshape(1, 256)
    wf = f32(inp["fox_wf"][0]); bfv = f32(inp["fox_bf"][0])
    x = inp["x"]
    in_maps = []
    for core in range(8):
        b, li = divmod(core, 4)
        relm = np.empty((2, 128, 1024), np.float32)
        for hl in range(2):
            relm[hl] = np.where(n >= 0, rel[bucket, 2 * li + hl], np.float32(NEG))
        wf_own = np.ascontiguousarray(wf[:, 4 * li:4 * li + 4].reshape(KC, 128, 4).transpose(1, 0, 2)).reshape(128, KC * 4)
        in_maps.append({
            "x_tm": np.ascontiguousarray(f32(x[b, li * T:(li + 1) * T, :])),
            "wshard": shards[li], "gb": gb, "relm": relm, "lqk": lqk, "subg": subg,
            "wf_own": wf_own, "bf_own": np.ascontiguousarray(bfv[4 * li:4 * li + 4].reshape(4, 1)),
            "ident": ident, "tri": tri, "mt": mt,
        })
    return in_maps


def run(cfg, inp, trace=False):
    nc = build_program(cfg)
    in_maps = pack_inputs(cfg, inp)
    res = run_bass_kernel_spmd(nc, in_maps, core_ids=list(range(8)))
    out = np.empty((2, cfg.S, cfg.D), np.float32)
    for core in range(8):
        b, li = divmod(core, 4)
        out[b, li * cfg.T:(li + 1) * cfg.T] = res.results[core]["out"]
    return out


def kernel(**inputs):
    cfg = Cfg()
    return run(cfg, inputs)
```

```python
from contextlib import ExitStack
import math
from concourse.bass_utils import run_bass_kernel_spmd
import numpy as np
import concourse.bass as bass
import concourse.mybir as mybir

F32, BF16 = mybir.dt.float32, mybir.dt.bfloat16
AF = mybir.ActivationFunctionType
ALU = mybir.AluOpType
AX = mybir.AxisListType


def _merge(dst, src):
    for k, v in src.items():
        if dst.get(k, (None, 0))[1] < v[1]:
            dst[k] = v


class Buf:
    def __init__(self, name):
        self.name = name
        self.prev = {}
        self.writers = {}
        self.readers = {}
        self.sem = None
        self.cnt = 0

    def new_gen(self):
        p = {}
        _merge(p, self.prev)
        _merge(p, self.writers)
        _merge(p, self.readers)
        self.prev = p
        self.writers = {}
        self.readers = {}


class KB:
    def __init__(self, nc, stack):
        self.nc = nc
        self.stack = stack
        self.engs = {"pe": nc.tensor, "act": nc.scalar, "dve": nc.vector, "pool": nc.gpsimd, "sp": nc.sync}
        self.esem = {}
        self.ecnt = {}
        for e in ("pe", "act", "dve", "pool"):
            self.esem[e] = stack.enter_context(nc.semaphore("s_" + e))
            self.ecnt[e] = 0
        self.waited = {e: {} for e in self.engs}
        self.dma_bufs = []
        self.sem_pool = []
        self.nsem = 4
        self.nwait = 0

    def buf(self, name):
        return Buf(name)

    def bufs(self, name, n):
        return [Buf(f"{name}{i}") for i in range(n)]

    def _sem_for(self, b):
        if b.sem is None:
            if self.sem_pool:
                b.sem, b.cnt = self.sem_pool.pop()
            else:
                b.sem = self.stack.enter_context(self.nc.semaphore("d_" + b.name))
                b.cnt = 0
                self.nsem += 1
            self.dma_bufs.append(b)
        return b.sem

    def release_sems(self, keep=()):
        rest = []
        for b in self.dma_bufs:
            if any(b is k for k in keep):
                rest.append(b)
            else:
                self.sem_pool.append((b.sem, b.cnt))
                b.sem = None
        self.dma_bufs = rest

    def _wait(self, eng, deps, skip_sem=None):
        w = self.waited[eng]
        for k, (sem, val) in deps.items():
            if skip_sem is not None and sem is skip_sem:
                continue
            if w.get(k, 0) < val:
                self.engs[eng].wait_ge(sem, val)
                w[k] = val
                self.nwait += 1

    def _deps(self, reads, writes, parts, rw):
        deps = {}
        for b in reads:
            _merge(deps, b.writers)
        for b in writes:
            b.new_gen()
            _merge(deps, b.prev)
        for b in parts:
            _merge(deps, b.prev)
        for b in rw:
            _merge(deps, b.prev)
            _merge(deps, b.writers)
            _merge(deps, b.readers)
        return deps

    def _update(self, ev, reads, writes, parts, rw):
        key = id(ev[0])
        for b in reads:
            _merge(b.readers, {key: ev})
        for b in list(writes) + list(parts):
            _merge(b.writers, {key: ev})
        for b in rw:
            b.prev = {}
            b.writers = {key: ev}
            b.readers = {}

    def op(self, eng, fn, reads=(), writes=(), parts=(), rw=()):
        deps = self._deps(reads, writes, parts, rw)
        self._wait(eng, deps, skip_sem=self.esem["pe"] if eng == "pe" else None)
        ins = fn(self.engs[eng])
        self.ecnt[eng] += 1
        ins.then_inc(self.esem[eng], 1)
        ev = (self.esem[eng], self.ecnt[eng])
        self._update(ev, reads, writes, parts, rw)
        return ins

    def dma(self, q, out, in_, sembuf, reads=(), writes=(), parts=(), rw=(), **kw):
        deps = self._deps(reads, writes, parts, rw)
        self._wait(q, deps)
        sem = self._sem_for(sembuf)
        ins = self.engs[q].dma_start(out=out, in_=in_, **kw)
        sembuf.cnt += 16
        ins.then_inc(sem, 16)
        ev = (sem, sembuf.cnt)
        self._update(ev, reads, writes, parts, rw)
        return ins

    def collective(self, kind, rg, in_ap, out_ap, sembuf, reads=(), writes=()):
        deps = self._deps(reads, writes, (), ())
        self._wait("pool", deps)
        sem = self._sem_for(sembuf)
        in_ap = in_ap.opt() if hasattr(in_ap, "opt") else in_ap
        out_ap = out_ap.opt() if hasattr(out_ap, "opt") else out_ap
        ins = self.engs["pool"].collective_compute(kind, ALU.bypass, replica_groups=rg, ins=[in_ap], outs=[out_ap])
        sembuf.cnt += 1
        ins.then_inc(sem)
        ev = (sem, sembuf.cnt)
        self._update(ev, reads, writes, (), ())
        return ins

    def all_events(self):
        deps = {}
        for e in ("pe", "act", "dve", "pool"):
            if self.ecnt[e] > 0:
                deps[id(self.esem[e])] = (self.esem[e], self.ecnt[e])
        for b in self.dma_bufs:
            if b.cnt > 0:
                deps[id(b.sem)] = (b.sem, b.cnt)
        return deps

    def barrier(self, engines=("pe", "act", "dve", "pool", "sp")):
        deps = self.all_events()
        for e in engines:
            self._wait(e, deps)

    def wait_all(self, eng):
        self._wait(eng, self.all_events())


ALPHA = 4 ** 0.25
LN_EPS = 1e-5
EPS_P = LN_EPS / (ALPHA * ALPHA)
NEG = -30000.0
RG = [[0, 1, 2, 3], [4, 5, 6, 7]]


class Cfg:
    def __init__(self, **kw):
        self.D = 2048; self.F = 5632; self.S = 16384; self.TB = 1024
        self.__dict__.update(kw)
        self.T = self.S // 4
        self.TB = min(self.TB, self.T)
        self.KC = self.D // 128
        self.FC = self.F // 128
        self.NJG = self.FC // 4
        self.NQT = self.S // 512
        self.NKT = self.S // 128
        self.rb = {}
        n = 0
        for l in range(2):
            for name, cnt in (("gu1", self.FC), ("d1", (4 * self.NJG) // 2), ("qkv", 24), ("wo", 8),
                              ("gu2", self.FC), ("d2", (4 * self.NJG) // 2)):
                self.rb[(l, name)] = n
                n += cnt
        self.NRB = ((n + 3) // 4) * 4
        self.K = self.NRB // 4


def barrier(kb, exclude=()):
    deps = kb.all_events()
    for b in exclude:
        if b.sem is not None:
            deps.pop(id(b.sem), None)
    for e in ("pe", "act", "dve", "pool", "sp"):
        kb._wait(e, deps)


class Prog:
    def __init__(self, nc, kb, cfg):
        self.nc, self.kb, self.cfg = nc, kb, cfg
        self.uid = 0
        self.AGW = None
        self.rb_off = 0

    def name(self, s):
        self.uid += 1
        return f"{s}_{self.uid}"

    def sb(self, st, n, shape, dt):
        return st.enter_context(self.nc.sbuf_tensor(self.name(n), shape, dt))

    def ps(self, st, n, shape, dt):
        return st.enter_context(self.nc.psum_tensor(self.name(n), shape, dt))

    def bar(self):
        keep = [self.AGW] if self.AGW is not None else []
        barrier(self.kb, exclude=keep)
        self.kb.release_sems(keep=keep)

    def need_rb(self, eng, rb_end):
        cnt = min((rb_end + self.rb_off + 3) // 4, self.AGW.cnt)
        self.kb._wait(eng, {id(self.AGW.sem): (self.AGW.sem, cnt)})


def phase0_weights(P, wshard, shard_bf, blobs):
    nc, kb, cfg = P.nc, P.kb, P.cfg
    P.AGW = kb.buf("AGW")
    with ExitStack() as st:
        f = [P.sb(st, "w0f", [128, 4096], F32) for _ in range(2)]; Fb = kb.bufs(P.name("W0F"), 2)
        b = [P.sb(st, "w0b", [128, 4096], BF16) for _ in range(2)]; Bb = kb.bufs(P.name("W0B"), 2)
        for k in range(cfg.K):
            s = k % 2
            kb.dma("sp", f[s][:], wshard[k], Fb[s], writes=[Fb[s]])
            kb.op("dve", lambda e: e.tensor_copy(out=b[s][:], in_=f[s][:]), reads=[Fb[s]], writes=[Bb[s]])
            SH = kb.buf(P.name("SH"))
            kb.dma("pool", shard_bf[k], b[s][:], Bb[s], reads=[Bb[s]], writes=[SH])
            KL = cfg.K // 2
            kb.collective("AllGather", RG, shard_bf[k], blobs[k // KL][(k % KL) * 512:(k % KL + 1) * 512, :], P.AGW, reads=[SH])
        P.bar()


def ln_sweep(P, st, y_scr, res_src, g_row, b_row, res_dst, xT_dst, r0, ntiles, ident_bf, IDB, plain=False):
    nc, kb, cfg = P.nc, P.kb, P.cfg
    D, KC = cfg.D, cfg.KC
    SD, AD = nc.vector.BN_STATS_DIM, nc.vector.BN_AGGR_DIM
    NCH = D // 512
    x_t = [P.sb(st, "x_t", [128, D], F32) for _ in range(2)]; XT = kb.bufs(P.name("XT"), 2)
    xb = [P.sb(st, "xb", [128, D], BF16) for _ in range(2)]; XB = kb.bufs(P.name("XB"), 2)
    xTo = [P.sb(st, "xTo", [128, KC, 512], BF16) for _ in range(2)]; XTO = kb.bufs(P.name("XTO"), 2)
    pT = [P.ps(st, "pT", [128, D], BF16) for _ in range(2)]; PT = kb.bufs(P.name("PT"), 2)
    if not plain:
        y_t = [P.sb(st, "y_t", [128, D], F32) for _ in range(2)]; YT = kb.bufs(P.name("YT"), 2)
        g_t = P.sb(st, "g_t", [128, D], F32); GT = kb.buf(P.name("GT"))
        b_t = P.sb(st, "b_t", [128, D], F32); BT = kb.buf(P.name("BT"))
        stt = [P.sb(st, "stt", [128, NCH, SD], F32) for _ in range(2)]; STT = kb.bufs(P.name("STT"), 2)
        mv = [P.sb(st, "mv", [128, AD + 2], F32) for _ in range(2)]; MV = kb.bufs(P.name("MV"), 2)
        kb.dma("sp", g_t[:], g_row.partition_broadcast(128), GT, writes=[GT])
        kb.dma("sp", b_t[:], b_row.partition_broadcast(128), BT, writes=[BT])

    def load(i):
        s = i % 2
        r = r0 + i * 128
        if not plain:
            kb.dma("sp", y_t[s][:], y_scr[r:r + 128, :], YT[s], writes=[YT[s]])
        kb.dma("sp", x_t[s][:], res_src[r:r + 128, :], XT[s], writes=[XT[s]])

    load(0)
    for i in range(ntiles):
        s = i % 2
        if i + 1 < ntiles:
            load(i + 1)
        r = r0 + i * 128
        xt = x_t[s]
        if not plain:
            yt = y_t[s]
            kb.op("dve", lambda e: e.tensor_tensor(out=yt[:], in0=yt[:], in1=xt[:], op=ALU.add), reads=[XT[s]], rw=[YT[s]])
            STT[s].new_gen()
            for c in range(NCH):
                kb.op("dve", lambda e, c=c: e.bn_stats(out=stt[s][:, c, :], in_=yt[:, c * 512:(c + 1) * 512]),
                      reads=[YT[s]], parts=[STT[s]])
            m = mv[s]
            kb.op("dve", lambda e: e.bn_aggr(out=m[:, 0:AD], in_=stt[s][:]), reads=[STT[s]], writes=[MV[s]])
            kb.op("act", lambda e: e.activation(out=m[:, AD:AD + 1], in_=m[:, 1:2], func=AF.Ln, bias=EPS_P), rw=[MV[s]])
            kb.op("act", lambda e: e.activation(out=m[:, AD:AD + 1], in_=m[:, AD:AD + 1], func=AF.Exp, scale=-0.5), rw=[MV[s]])
            kb.op("dve", lambda e: e.scalar_tensor_tensor(out=m[:, AD + 1:AD + 2], in0=m[:, 0:1], scalar=-1.0,
                                                          in1=m[:, AD:AD + 1], op0=ALU.mult, op1=ALU.mult), rw=[MV[s]])
            kb.op("act", lambda e: e.activation(out=xt[:], in_=yt[:], func=AF.Identity, bias=m[:, AD + 1:AD + 2],
                                                scale=m[:, AD:AD + 1]), reads=[YT[s], MV[s]], writes=[XT[s]])
            kb.op("pool", lambda e: e.tensor_tensor(out=xt[:], in0=xt[:], in1=g_t[:], op=ALU.mult), reads=[GT], rw=[XT[s]])
            kb.op("pool", lambda e: e.tensor_tensor(out=xt[:], in0=xt[:], in1=b_t[:], op=ALU.add), reads=[BT], rw=[XT[s]])
            kb.dma("pool", res_dst[r:r + 128, :], xt[:], XT[s], reads=[XT[s]])
        if xT_dst is not None:
            kb.op("dve", lambda e: e.tensor_copy(out=xb[s][:], in_=xt[:]), reads=[XT[s]], writes=[XB[s]])
            PT[s].new_gen()
            for c in range(KC):
                kb.op("pe", lambda e, c=c: e.transpose(pT[s][:, c * 128:(c + 1) * 128], xb[s][:, c * 128:(c + 1) * 128],
                                                       ident_bf[:]), reads=[XB[s], IDB], parts=[PT[s]])
            g4, q4 = divmod(i, 4)
            so = g4 % 2
            if q4 == 0:
                XTO[so].new_gen()
            kb.op("act", lambda e: e.activation(out=xTo[so][:, :, q4 * 128:(q4 + 1) * 128],
                                                in_=pT[s][:].rearrange("p (c t) -> p c t", t=128), func=AF.Copy),
                  reads=[PT[s]], parts=[XTO[so]])
            if q4 == 3 or i == ntiles - 1:
                w = (q4 + 1) * 128
                c0 = r0 + g4 * 512
                kb.dma("pool", xT_dst[:, c0:c0 + w].rearrange("(c p) t -> p c t", p=128), xTo[so][:, :, 0:w],
                       XTO[so], reads=[XTO[so]])


def ffn_phase(P, blob, rb_gu, rb_d, g_row, b_row, xT_src, res_src, res_dst, xT_dst, y_scr, ident_bf, IDB):
    nc, kb, cfg = P.nc, P.kb, P.cfg
    D, KC, FC, T, TB, NJG = cfg.D, cfg.KC, cfg.FC, cfg.T, cfg.TB, cfg.NJG
    NPASS, NH, NTT = T // TB, TB // 512, TB // 128
    NCB = D // 512
    yscale = 0.5 / ALPHA
    P.need_rb("sp", rb_d + (NCB * NJG) // 2)
    with ExitStack() as st:
        xT = P.sb(st, "xT", [128, KC, TB], BF16); XTB = kb.buf(P.name("XTB"))
        wgb = [P.sb(st, "wgb", [128, 4096], BF16) for _ in range(3)]; WGB = kb.bufs(P.name("WGB"), 3)
        wdb = [P.sb(st, "wdb", [128, 2048], BF16) for _ in range(3)]; WDB = kb.bufs(P.name("WDB"), 3)
        yb = [P.sb(st, "yb", [128, 512], F32) for _ in range(4)]; YB = kb.bufs(P.name("YB"), 4)
        sg = [P.sb(st, "sg", [128, 512], F32) for _ in range(2)]; SG = kb.bufs(P.name("SG"), 2)
        for ps_i in range(NPASS):
            t0 = ps_i * TB
            with ExitStack() as st2:
                hT = P.sb(st2, "hT", [128, FC * TB], BF16); HT = kb.buf(P.name("HT"))
                acc = [P.ps(st2, "acc", [128, 512], F32) for _ in range(8)]; ACC = kb.bufs(P.name("ACC"), 8)
                kb.dma("sp", xT[:], xT_src[:, t0:t0 + TB].rearrange("(c p) t -> p c t", p=128), XTB, writes=[XTB])
                tasks = [("gu", j) for j in range(FC)] + [("d", cb, jg) for cb in range(NCB) for jg in range(NJG)]

                def load(task):
                    if task[0] == "gu":
                        j = task[1]; s = j % 3
                        r = (rb_gu + j) * 128
                        kb.dma("sp", wgb[s][:], blob[r:r + 128, :], WGB[s], writes=[WGB[s]])
                    else:
                        _, cb, jg = task; tl = cb * NJG + jg; s = tl % 3
                        r = (rb_d + tl // 2) * 128
                        kb.dma("sp", wdb[s][:], blob[r:r + 128, (tl % 2) * 2048:(tl % 2 + 1) * 2048], WDB[s], writes=[WDB[s]])

                load(tasks[0]); load(tasks[1])
                ev_i = 0
                for ti, task in enumerate(tasks):
                    if ti + 2 < len(tasks):
                        load(tasks[ti + 2])
                    if task[0] == "gu":
                        j = task[1]; s = j % 3
                        for hb in range(NH):
                            k = (j * NH + hb) % 2
                            gp, up = acc[2 * k], acc[2 * k + 1]
                            GP, UP = ACC[2 * k], ACC[2 * k + 1]
                            GP.new_gen(); UP.new_gen()
                            for c in range(KC):
                                kb.op("pe", lambda e, c=c: e.matmul(gp[:], lhsT=wgb[s][:, c * 128:(c + 1) * 128],
                                                                    rhs=xT[:, c, hb * 512:(hb + 1) * 512],
                                                                    start=(c == 0), stop=(c == KC - 1)),
                                      reads=[WGB[s], XTB], parts=[GP])
                            for c in range(KC):
                                kb.op("pe", lambda e, c=c: e.matmul(up[:], lhsT=wgb[s][:, (KC + c) * 128:(KC + c + 1) * 128],
                                                                    rhs=xT[:, c, hb * 512:(hb + 1) * 512],
                                                                    start=(c == 0), stop=(c == KC - 1)),
                                      reads=[WGB[s], XTB], parts=[UP])
                            kb.op("act", lambda e: e.activation(out=sg[k][:], in_=gp[:], func=AF.Silu), reads=[GP], writes=[SG[k]])
                            o0 = j * TB + hb * 512
                            kb.op("dve", lambda e: e.tensor_tensor(out=hT[:, o0:o0 + 512], in0=sg[k][:], in1=up[:], op=ALU.mult),
                                  reads=[SG[k], UP], parts=[HT])
                    else:
                        _, cb, jg = task; s = (cb * NJG + jg) % 3
                        for jj in range(4):
                            j = jg * 4 + jj
                            for tt in range(NTT):
                                if j == 0:
                                    ACC[tt].new_gen()
                                kb.op("pe", lambda e, tt=tt, j=j, jj=jj: e.matmul(
                                    acc[tt][:], lhsT=hT[:, j * TB + tt * 128:j * TB + (tt + 1) * 128],
                                    rhs=wdb[s][:, jj * 512:(jj + 1) * 512], start=(j == 0), stop=(j == FC - 1)),
                                    reads=[WDB[s], HT], parts=[ACC[tt]])
                        if jg == NJG - 1:
                            for tt in range(NTT):
                                ys = ev_i % 4; ev_i += 1
                                kb.op("act", lambda e, tt=tt, ys=ys: e.activation(out=yb[ys][:], in_=acc[tt][:], func=AF.Identity, scale=yscale),
                                      reads=[ACC[tt]], writes=[YB[ys]])
                                r = t0 + tt * 128
                                kb.dma("pool", y_scr[r:r + 128, cb * 512:(cb + 1) * 512], yb[ys][:], YB[ys], reads=[YB[ys]])
                P.bar()
            with ExitStack() as st3:
                ln_sweep(P, st3, y_scr, res_src, g_row, b_row, res_dst, xT_dst, t0, NTT, ident_bf, IDB)
                P.bar()


def gather_chunks(P, srcs, dsts):
    kb = P.kb
    AGX = kb.buf(P.name("AGX"))
    for s_ap, d_ap in zip(srcs, dsts):
        kb.collective("AllGather", RG, s_ap, d_ap, AGX)
    P.bar()


def qkv_phase(P, blob, rb_qkv, xg_dst, qT_scr, kT_scr, v_scr, fox, wf_own, bf_own, lf_scr, pid_li):
    nc, kb, cfg = P.nc, P.kb, P.cfg
    KC, S, T = cfg.KC, cfg.S, cfg.T
    NB = S // 512
    qscale = 128 ** -0.5
    P.need_rb("pool", rb_qkv + 24)
    with ExitStack() as st:
        w = P.sb(st, "wqkv", [128, KC * 1536], BF16); W = kb.buf(P.name("WQKV"))
        WOWN = kb.buf(P.name("WOWN"))
        pid_li = nc.gpsimd.partition_id() % 4
        kb.dma("pool", P.wown[:, :], blob[bass.ds(pid_li * 768 + rb_qkv * 128, 768), :], WOWN, writes=[WOWN])
        for m in range(6):
            kb.dma("pool", w[:, m * 4096:(m + 1) * 4096], P.wown[m * 128:(m + 1) * 128, :], W, reads=[WOWN], parts=[W])
        xT = [P.sb(st, "qx", [128, KC, 512], BF16) for _ in range(2)]; XTB = kb.bufs(P.name("QX"), 2)
        qk_o = [P.sb(st, "qko", [128, 8, 512], BF16) for _ in range(2)]; QKO = kb.bufs(P.name("QKO"), 2)
        v_o = [P.sb(st, "vo", [128, 4, 512], BF16) for _ in range(2)]; VO = kb.bufs(P.name("VO"), 2)
        acc = [P.ps(st, "qacc", [128, 512], F32) for _ in range(4)]; ACC = kb.bufs(P.name("QACC"), 4)
        if fox:
            wff = P.sb(st, "wff", [128, KC * 4], F32); WFF = kb.buf(P.name("WFF"))
            wfb = P.sb(st, "wfb", [128, KC * 4], BF16); WFB = kb.buf(P.name("WFB"))
            bfc = P.sb(st, "bfc", [4, 2], F32); BFC = kb.buf(P.name("BFC"))
            kb.dma("sp", wff[:], wf_own[:, :], WFF, writes=[WFF])
            kb.dma("sp", bfc[:, 0:1], bf_own[:, :], BFC, writes=[BFC])
            kb.op("dve", lambda e: e.tensor_copy(out=wfb[:], in_=wff[:]), reads=[WFF], writes=[WFB])
            kb.op("dve", lambda e: e.tensor_scalar(out=bfc[:, 1:2], in0=bfc[:, 0:1], scalar1=-1.0, scalar2=None, op0=ALU.mult), rw=[BFC])
            facc = P.ps(st, "facc", [4, 512], F32); FACC = kb.buf(P.name("FACC"))
            lf = [P.sb(st, "lf", [4, 512], F32) for _ in range(2)]; LF = kb.bufs(P.name("LF"), 2)

        def load(b):
            s = b % 2
            r, t0 = divmod(b * 512, T)
            kb.dma("sp", xT[s][:], xg_dst[:, r * 128:(r + 1) * 128, t0:t0 + 512].rearrange("c p t -> p c t"), XTB[s],
                   writes=[XTB[s]])

        load(0)
        ai = 0
        for b in range(NB):
            s = b % 2
            if b + 1 < NB:
                load(b + 1)
            x = xT[s]
            QKO[s].new_gen()
            for o in range(8):
                a = ai % 4; ai += 1
                ACC[a].new_gen()
                for c in range(KC):
                    kb.op("pe", lambda e, c=c: e.matmul(acc[a][:], lhsT=w[:, c * 1536 + o * 128:c * 1536 + (o + 1) * 128],
                                                        rhs=x[:, c, :], start=(c == 0), stop=(c == KC - 1)),
                          reads=[W, XTB[s]], parts=[ACC[a]])
                eng = "act" if o % 2 == 0 else "dve"
                sc = qscale if o < 4 else 1.0
                if eng == "act":
                    kb.op("act", lambda e: e.activation(out=qk_o[s][:, o, :], in_=acc[a][:], func=AF.Identity, scale=sc),
                          reads=[ACC[a]], parts=[QKO[s]])
                else:
                    kb.op("dve", lambda e: e.tensor_scalar(out=qk_o[s][:, o, :], in0=acc[a][:], scalar1=sc, scalar2=None, op0=ALU.mult),
                          reads=[ACC[a]], parts=[QKO[s]])
            kb.dma("pool", qT_scr[:, :, b * 512:(b + 1) * 512].rearrange("u p t -> p u t"), qk_o[s][:, 0:4, :], QKO[s], reads=[QKO[s]])
            kb.dma("pool", kT_scr[:, :, b * 512:(b + 1) * 512].rearrange("u p t -> p u t"), qk_o[s][:, 4:8, :], QKO[s], reads=[QKO[s]])
            VO[s].new_gen()
            for tt in range(4):
                a = ai % 4; ai += 1
                ACC[a].new_gen()
                for c in range(KC):
                    kb.op("pe", lambda e, c=c: e.matmul(acc[a][:], lhsT=x[:, c, tt * 128:(tt + 1) * 128],
                                                        rhs=w[:, c * 1536 + 1024:c * 1536 + 1536], start=(c == 0), stop=(c == KC - 1)),
                          reads=[W, XTB[s]], parts=[ACC[a]])
                if tt % 2 == 0:
                    kb.op("act", lambda e: e.activation(out=v_o[s][:, tt, :], in_=acc[a][:], func=AF.Copy), reads=[ACC[a]], parts=[VO[s]])
                else:
                    kb.op("dve", lambda e: e.tensor_copy(out=v_o[s][:, tt, :], in_=acc[a][:]), reads=[ACC[a]], parts=[VO[s]])
            kb.dma("pool", v_scr[b * 512:(b + 1) * 512, :].rearrange("(tt p) e -> p tt e", p=128), v_o[s][:], VO[s], reads=[VO[s]])
            if fox:
                FACC.new_gen()
                for c in range(KC):
                    kb.op("pe", lambda e, c=c: e.matmul(facc[:], lhsT=wfb[:, c * 4:(c + 1) * 4], rhs=x[:, c, :],
                                                        start=(c == 0), stop=(c == KC - 1)), reads=[WFB, XTB[s]], parts=[FACC])
                kb.op("act", lambda e: e.activation(out=lf[s][:], in_=facc[:], func=AF.Exp, bias=bfc[:, 1:2], scale=-1.0),
                      reads=[FACC, BFC], writes=[LF[s]])
                kb.op("act", lambda e: e.activation(out=lf[s][:], in_=lf[s][:], func=AF.Ln, bias=1.0), rw=[LF[s]])
                kb.dma("pool", lf_scr[:, b * 512:(b + 1) * 512], lf[s][:], LF[s], reads=[LF[s]])
        P.bar()


def fox_cumsum(P, lf_scr, negc_scr, mt_dram):
    nc, kb, cfg = P.nc, P.kb, P.cfg
    SEG = cfg.S // 32
    with ExitStack() as st:
        L = P.sb(st, "cL", [128, SEG], F32); LB = kb.buf(P.name("CL"))
        Z = P.sb(st, "cZ", [128, SEG], F32); ZB = kb.buf(P.name("CZ"))
        C = P.sb(st, "cC", [128, SEG], F32); CB = kb.buf(P.name("CC"))
        MT = P.sb(st, "cMT", [128, 128], F32); MTB = kb.buf(P.name("CMT"))
        off = P.sb(st, "coff", [128, 1], F32); OFF = kb.buf(P.name("COFF"))
        pp = P.ps(st, "cpp", [128, 2], F32); PP = kb.buf(P.name("CPP"))
        kb.dma("sp", L[:], lf_scr.rearrange("u (g s) -> (u g) s", s=SEG), LB, writes=[LB])
        kb.dma("sp", MT[:], mt_dram[:, :], MTB, writes=[MTB])
        kb.op("dve", lambda e: e.memset(Z[:], 0.0), writes=[ZB])
        kb.op("dve", lambda e: e.tensor_tensor_scan(out=C[:], data0=Z[:], data1=L[:], initial=0.0, op0=ALU.add, op1=ALU.add),
              reads=[ZB, LB], writes=[CB])
        kb.op("pe", lambda e: e.matmul(pp[:, 0:1], lhsT=MT[:, :], rhs=C[:, SEG - 1:SEG], start=True, stop=True),
              reads=[MTB, CB], writes=[PP])
        kb.op("dve", lambda e: e.tensor_copy(out=off[:], in_=pp[:, 0:1]), reads=[PP], writes=[OFF])
        kb.op("dve", lambda e: e.tensor_scalar(out=C[:], in0=C[:], scalar1=off[:, 0:1], scalar2=None, op0=ALU.add), reads=[OFF], rw=[CB])
        kb.dma("pool", negc_scr.rearrange("u (g s) -> (u g) s", s=SEG), C[:], CB, reads=[CB])
        P.bar()


def attention_phase(P, fox, qT_scr, kT_scr, v_scr, oT_src, ident_bf, IDB, ident_f, IDF, tri_dram, negc_scr,
                    relm, lqk, subg, lam_init):
    nc, kb, cfg = P.nc, P.kb, P.cfg
    S, T, NQT, NKT = cfg.S, cfg.T, cfg.NQT, cfg.NKT
    VW = 128 if fox else 256
    NHEAD = 4 if fox else 2
    with ExitStack() as st:
        KTs = [P.sb(st, "aKT", [128, S], BF16) for _ in range(1 if fox else 2)]; KTBs = kb.bufs(P.name("AKT"), 1 if fox else 2)
        V = P.sb(st, "aV", [128, NKT, VW + 1], BF16); VB = kb.buf(P.name("AV"))
        NQS = 2 if fox else 3
        QT = [P.sb(st, "aQT", [128, 512], BF16) for _ in range(NQS)]; QTB = kb.bufs(P.name("AQT"), NQS)
        PTl = [P.sb(st, "aP", [128, 512], BF16) for _ in range(4)]; PB = kb.bufs(P.name("AP"), 4)
        NS = 2 if fox else 3
        TMP = [P.sb(st, "aTMP", [128, 512], F32) for _ in range(NS)]; TMPB = kb.bufs(P.name("ATMP"), NS)
        sps = [P.ps(st, "aS", [128, 512], F32) for _ in range(NS)]; SPS = kb.bufs(P.name("AS"), NS)
        ops_ = [P.ps(st, "aO", [128, 512], F32) for _ in range(4)]; OPS = kb.bufs(P.name("AO"), 4)
        tps = P.ps(st, "aT", [128, 1024], BF16); TPS = kb.buf(P.name("AT"))
        osb = [P.sb(st, "aosb", [128, VW + 1], F32) for _ in range(2)]; OSB = kb.bufs(P.name("AOSB"), 2)
        rl = [P.sb(st, "arl", [128, 4], F32) for _ in range(2)]; RL = kb.bufs(P.name("ARL"), 2)
        onb = [P.sb(st, "aonb", [128, VW], BF16) for _ in range(2)]; ONB = kb.bufs(P.name("AONB"), 2)
        oT = [P.sb(st, "aoT", [128, VW // 128, 512], BF16) for _ in range(2)]; OTB = kb.bufs(P.name("AOT"), 2)
        if fox:
            tri_f = P.sb(st, "atrif", [128, 128], F32); TRIF = kb.buf(P.name("ATRIF"))
            tri = P.sb(st, "atri", [128, 128], BF16); TRI = kb.buf(P.name("ATRI"))
            kb.dma("sp", tri_f[:], tri_dram[:, :], TRIF, writes=[TRIF])
            kb.op("dve", lambda e: e.tensor_copy(out=tri[:], in_=tri_f[:]), reads=[TRIF], writes=[TRI])
            ncr = P.sb(st, "ancr", [NKT, 128], F32); NCR = kb.buf(P.name("ANCR"))
            negck = P.sb(st, "anegck", [128, NKT], F32); NEGCK = kb.buf(P.name("ANEGCK"))
            cbc = [P.sb(st, "acbc", [128, 512], F32) for _ in range(2)]; CBC = kb.bufs(P.name("ACBC"), 2)
            ncp = P.ps(st, "ancp", [128, NKT], F32); NCP = kb.buf(P.name("ANCP"))
        else:
            M = P.sb(st, "aM", [128, 1024], F32); MB = kb.buf(P.name("AM"))
            o1n = P.sb(st, "ao1n", [128, 4, 256], F32); O1N = kb.bufs(P.name("AO1N"), 4)
            od = [P.sb(st, "aod", [128, 256], F32) for _ in range(2)]; OD = kb.bufs(P.name("AOD"), 2)
            sqj = P.sb(st, "asqj", [128, 256], F32); SQJ = kb.buf(P.name("ASQJ"))
            lq = P.sb(st, "alq", [128, 4, 128], F32); LQ = kb.buf(P.name("ALQ"))
            lam = P.sb(st, "alam", [128, 8], F32); LAM = kb.buf(P.name("ALAM"))
            gsb = P.sb(st, "agsb", [128, 256], F32); GSB = kb.buf(P.name("AGSB"))
            for i in range(4):
                kb.dma("sp", lq[:, i, :], lqk[i:i + 1, :].partition_broadcast(128), LQ, parts=[LQ])
            kb.dma("sp", gsb[:], subg[0:1, :].partition_broadcast(128), GSB, writes=[GSB])
            kb.op("dve", lambda e: e.tensor_tensor(out=lq[:, 0, :], in0=lq[:, 0, :], in1=lq[:, 1, :], op=ALU.mult), rw=[LQ])
            kb.op("dve", lambda e: e.tensor_tensor(out=lq[:, 2, :], in0=lq[:, 2, :], in1=lq[:, 3, :], op=ALU.mult), rw=[LQ])
            kb.op("dve", lambda e: e.reduce_sum(out=lam[:, 0:1], in_=lq[:, 0, :], axis=AX.X), reads=[LQ], writes=[LAM])
            kb.op("dve", lambda e: e.reduce_sum(out=lam[:, 1:2], in_=lq[:, 2, :], axis=AX.X), reads=[LQ], rw=[LAM])
            kb.op("act", lambda e: e.activation(out=lam[:, 2:4], in_=lam[:, 0:2], func=AF.Exp), rw=[LAM])
            kb.op("dve", lambda e: e.tensor_tensor(out=lam[:, 4:5], in0=lam[:, 3:4], in1=lam[:, 2:3], op=ALU.subtract), rw=[LAM])
            kb.op("dve", lambda e: e.tensor_scalar(out=lam[:, 5:6], in0=lam[:, 4:5], scalar1=-lam_init, scalar2=None, op0=ALU.add), rw=[LAM])
            kb.op("dve", lambda e: e.tensor_scalar(out=gsb[:], in0=gsb[:], scalar1=1.0 - lam_init, scalar2=None, op0=ALU.mult), rw=[GSB])
            neglam = lam[:, 5:6]
            ssum = [P.sb(st, "assum", [128, 2], F32) for _ in range(2)]; SSUM = kb.bufs(P.name("ASSUM"), 2)

        pi = 0
        si = 0
        fi = 0
        NM = 1 if fox else 2
        for head in range(NHEAD):
            for mp_ in range(NM):
                kb.dma("sp", KTs[mp_][:], kT_scr[head * NM + mp_], KTBs[mp_], writes=[KTBs[mp_]])
            VB.new_gen()
            vsrc = v_scr[:, head * VW:(head + 1) * VW].rearrange("(kt p) e -> p kt e", p=128)
            nvs = max(1, NKT // 32)
            for vi in range(nvs):
                k0, k1 = vi * (NKT // nvs), (vi + 1) * (NKT // nvs)
                kb.dma("sp", V[:, k0:k1, 0:VW], vsrc[:, k0:k1, :], VB, parts=[VB])
            kb.op("pool", lambda e: e.memset(V[:, :, VW:VW + 1], 1.0), parts=[VB])
            if fox:
                kb.dma("sp", ncr[:], negc_scr[head:head + 1, :].rearrange("o (kt p) -> (o kt) p", p=128), NCR, writes=[NCR])
                kb.op("pe", lambda e: e.transpose(ncp[:], ncr[:], ident_f[0:NKT, 0:NKT]), reads=[NCR, IDF], writes=[NCP])
                kb.op("dve", lambda e: e.tensor_copy(out=negck[:], in_=ncp[:]), reads=[NCP], writes=[NEGCK])
            else:
                kb.dma("sp", M[:], relm[head], MB, writes=[MB])
            units = [(t, mp) for t in range(NQT) for mp in range(NM)]
            steps = [(ui, kt) for ui, (t, mp) in enumerate(units) for kt in range(4 * t + 4)]
            score_slot = {}

            def emit_qload(ui):
                t, mp = units[ui]
                u = head * NM + mp
                qs = ui % NQS
                kb.dma("sp", QT[qs][:], qT_scr[u, :, t * 512:(t + 1) * 512], QTB[qs], writes=[QTB[qs]])
                if fox:
                    kb.dma("sp", cbc[qs][:], negc_scr[u:u + 1, t * 512:(t + 1) * 512].partition_broadcast(128), CBC[qs], writes=[CBC[qs]])

            def emit_score(sidx):
                nonlocal si
                ui, kt = steps[sidx]
                t, mp = units[ui]
                c0 = max(kt - 4 * t, 0) * 128
                qs = ui % NQS
                sp = si % NS; si += 1
                kb.op("pe", lambda e: e.matmul(sps[sp][:, c0:512], lhsT=KTs[mp][:, kt * 128:(kt + 1) * 128], rhs=QT[qs][:, c0:512],
                                               start=True, stop=True), reads=[KTBs[mp], QTB[qs]], writes=[SPS[sp]])
                score_slot[sidx] = sp

            LA = NS - 1
            emit_qload(0)
            if LA > 1 and len(units) > 1:
                emit_qload(1)
            for s0 in range(min(LA, len(steps))):
                emit_score(s0)
            sidx = 0
            for ui, (t, mp) in enumerate(units):
                u = head * NM + mp
                qs_ = ui % NQS
                nxt = ui + 1 if LA == 1 else ui + 2
                if nxt < len(units):
                    emit_qload(nxt)
                for q in range(4):
                    OPS[q].new_gen()
                nk = 4 * t + 4
                for kt in range(nk):
                    j = kt - 4 * t
                    c0 = max(j, 0) * 128
                    if sidx + LA < len(steps):
                        emit_score(sidx + LA)
                    sp_ = score_slot.pop(sidx); sidx += 1
                    pp_ = pi % 4; pi += 1
                    if fox:
                        tm = TMP[sp_]
                        kb.op("dve", lambda e: e.tensor_tensor(out=tm[:, c0:512], in0=sps[sp_][:, c0:512], in1=cbc[qs_][:, c0:512],
                                                               op=ALU.subtract), reads=[SPS[sp_], CBC[qs_]], writes=[TMPB[sp_]])
                        kb.op("act", lambda e: e.activation(out=PTl[pp_][:, c0:512], in_=tm[:, c0:512], func=AF.Exp,
                                                            bias=negck[:, kt:kt + 1]), reads=[TMPB[sp_], NEGCK], writes=[PB[pp_]])
                        if j >= 0:
                            kb.op("pool", lambda e: e.tensor_tensor(out=PTl[pp_][:, c0:c0 + 128], in0=PTl[pp_][:, c0:c0 + 128],
                                                                    in1=tri[:], op=ALU.mult), reads=[TRI], rw=[PB[pp_]])
                    else:
                        if j >= -1:
                            tm = TMP[sp_]
                            x0 = 384 - j * 128 + c0
                            kb.op("dve", lambda e: e.tensor_tensor(out=tm[:, c0:512], in0=sps[sp_][:, c0:512], in1=M[:, x0:x0 + 512 - c0],
                                                                   op=ALU.add), reads=[SPS[sp_], MB], writes=[TMPB[sp_]])
                            kb.op("act", lambda e: e.activation(out=PTl[pp_][:, c0:512], in_=tm[:, c0:512], func=AF.Exp),
                                  reads=[TMPB[sp_]], writes=[PB[pp_]])
                        else:
                            kb.op("act", lambda e: e.activation(out=PTl[pp_][:, :], in_=sps[sp_][:, :], func=AF.Exp,
                                                                bias=M[:, 1023:1024]), reads=[SPS[sp_], MB], writes=[PB[pp_]])
                    for q in range(max(j, 0), 4):
                        last = (kt == 4 * t + q)
                        kb.op("pe", lambda e, q=q: e.matmul(ops_[q][:, 0:VW + 1], lhsT=PTl[pp_][:, q * 128:(q + 1) * 128], rhs=V[:, kt, :],
                                                            start=(kt == 0), stop=last), reads=[PB[pp_], VB], parts=[OPS[q]])
                ot = fi % 2
                if fox or mp == 1:
                    OTB[ot].new_gen()
                for q in range(4):
                    f2 = (fi * 4 + q) % 2
                    ob, r_ = osb[f2], rl[f2]
                    kb.op("act", lambda e: e.activation(out=ob[:], in_=ops_[q][:, 0:VW + 1], func=AF.Copy), reads=[OPS[q]], writes=[OSB[f2]])
                    kb.op("dve", lambda e: e.reciprocal(out=r_[:, 0:1], in_=ob[:, VW:VW + 1]), reads=[OSB[f2]], writes=[RL[f2]])
                    if fox:
                        kb.op("dve", lambda e: e.tensor_scalar(out=onb[f2][:], in0=ob[:, 0:VW], scalar1=r_[:, 0:1], scalar2=None, op0=ALU.mult),
                              reads=[OSB[f2], RL[f2]], writes=[ONB[f2]])
                    elif mp == 0:
                        kb.op("dve", lambda e: e.tensor_scalar(out=o1n[:, q, :], in0=ob[:, 0:VW], scalar1=r_[:, 0:1], scalar2=None, op0=ALU.mult),
                              reads=[OSB[f2], RL[f2]], writes=[O1N[q]])
                        continue
                    else:
                        d_ = od[f2]
                        kb.op("dve", lambda e: e.tensor_scalar(out=r_[:, 1:2], in0=r_[:, 0:1], scalar1=neglam, scalar2=None, op0=ALU.mult),
                              reads=[LAM], rw=[RL[f2]])
                        kb.op("dve", lambda e: e.scalar_tensor_tensor(out=d_[:], in0=ob[:, 0:VW], scalar=r_[:, 1:2], in1=o1n[:, q, :],
                                                                      op0=ALU.mult, op1=ALU.add), reads=[OSB[f2], RL[f2], O1N[q]], writes=[OD[f2]])
                        ss_ = ssum[f2]
                        kb.op("act", lambda e: e.activation(out=sqj[:], in_=d_[:], func=AF.Square, accum_out=ss_[:, 0:1]),
                              reads=[OD[f2]], writes=[SQJ, SSUM[f2]])
                        kb.op("act", lambda e: e.activation(out=ss_[:, 1:2], in_=ss_[:, 0:1], func=AF.Ln, bias=LN_EPS, scale=1.0 / 256), rw=[SSUM[f2]])
                        kb.op("act", lambda e: e.activation(out=ss_[:, 1:2], in_=ss_[:, 1:2], func=AF.Exp, scale=-0.5), rw=[SSUM[f2]])
                        kb.op("dve", lambda e: e.scalar_tensor_tensor(out=onb[f2][:], in0=d_[:], scalar=ss_[:, 1:2], in1=gsb[:],
                                                                      op0=ALU.mult, op1=ALU.mult), reads=[OD[f2], SSUM[f2], GSB], writes=[ONB[f2]])
                    TPS.new_gen()
                    for c in range(VW // 128):
                        kb.op("pe", lambda e, c=c: e.transpose(tps[:, c * 128:(c + 1) * 128], onb[f2][:, c * 128:(c + 1) * 128], ident_bf[:]),
                              reads=[ONB[f2], IDB], parts=[TPS])
                    kb.op("act", lambda e: e.activation(out=oT[ot][:, :, q * 128:(q + 1) * 128],
                                                        in_=tps[:, 0:VW].rearrange("p (c t) -> p c t", t=128), func=AF.Copy),
                          reads=[TPS], parts=[OTB[ot]])
                if fox or mp == 1:
                    tc, tq = divmod(t * 512, T)
                    for c in range(VW // 128):
                        fc = u if fox else head * 2 + c
                        kb.dma("pool", oT_src[fc, tc, :, tq:tq + 512], oT[ot][:, c, :], OTB[ot], reads=[OTB[ot]])
                fi += 1
        P.bar()


def wo_phase(P, blob, rb_wo, oT_dst2, y_scr, pid_li):
    nc, kb, cfg = P.nc, P.kb, P.cfg
    KC, T, D = cfg.KC, cfg.T, cfg.D
    P.need_rb("sp", rb_wo + 8)
    with ExitStack() as st:
        w = P.sb(st, "wo", [128, KC * D], BF16); W = kb.buf(P.name("WO"))
        for m in range(8):
            r = (rb_wo + m) * 128
            kb.dma("sp", w[:, m * 4096:(m + 1) * 4096], blob[r:r + 128, :], W, parts=[W])
        oT = [P.sb(st, "woT", [128, KC, 512], BF16) for _ in range(2)]; OT = kb.bufs(P.name("WOT"), 2)
        acc = [P.ps(st, "wacc", [128, 512], F32) for _ in range(4)]; ACC = kb.bufs(P.name("WACC"), 4)
        yb = [P.sb(st, "wyb", [128, 512], F32) for _ in range(4)]; YB = kb.bufs(P.name("WYB"), 4)
        NG = T // 512

        OMINE = kb.buf(P.name("OMINE"))
        pid_sp = nc.sync.partition_id() % 4
        base = oT_dst2[bass.ds(pid_sp * 2048, 2048), :]
        for fc in range(4):
            kb.dma("sp", P.omine[fc], base[fc * 512:(fc + 1) * 512, :], OMINE, parts=[OMINE])

        def load(g):
            s = g % 2
            OT[s].new_gen()
            for fc in range(4):
                for r in range(4):
                    ci = r * 4 + fc
                    kb.dma("sp", oT[s][:, ci, :], P.omine[fc, r * 128:(r + 1) * 128, g * 512:(g + 1) * 512], OT[s],
                           reads=[OMINE], parts=[OT[s]])

        load(0)
        ai = 0
        for g in range(NG):
            s = g % 2
            if g + 1 < NG:
                load(g + 1)
            for tt in range(4):
                for cb in range(4):
                    a = ai % 4; ai += 1
                    ACC[a].new_gen()
                    for c in range(KC):
                        kb.op("pe", lambda e, c=c: e.matmul(acc[a][:], lhsT=oT[s][:, c, tt * 128:(tt + 1) * 128],
                                                            rhs=w[:, c * D + cb * 512:c * D + (cb + 1) * 512], start=(c == 0), stop=(c == KC - 1)),
                              reads=[W, OT[s]], parts=[ACC[a]])
                    kb.op("act", lambda e: e.activation(out=yb[a][:], in_=acc[a][:], func=AF.Identity, scale=1.0 / ALPHA),
                          reads=[ACC[a]], writes=[YB[a]])
                    r0 = g * 512 + tt * 128
                    kb.dma("pool", y_scr[r0:r0 + 128, cb * 512:(cb + 1) * 512], yb[a][:], YB[a], reads=[YB[a]])
        P.bar()


def build_program(cfg):
    nc = bass.Bass("TRN2", target_bir_lowering=False)
    D, T, S, K = cfg.D, cfg.T, cfg.S, cfg.K
    dt_in = lambda n, s: nc.dram_tensor(n, s, F32, kind="ExternalInput").ap()
    x_tm = dt_in("x_tm", [T, D])
    wshard = dt_in("wshard", [K, 128, 4096])
    gb = dt_in("gb", [12, D])
    relm = dt_in("relm", [2, 128, 1024])
    lqk = dt_in("lqk", [4, 128])
    subg = dt_in("subg", [1, 256])
    wf_own = dt_in("wf_own", [128, cfg.KC * 4])
    bf_own = dt_in("bf_own", [4, 1])
    ident = dt_in("ident", [128, 128])
    tri = dt_in("tri", [128, 128])
    mt = dt_in("mt", [128, 128])
    out = nc.dram_tensor("out", [T, D], F32, kind="ExternalOutput").ap()
    di = lambda n, s, d: nc.dram_tensor(n, s, d)
    RBL = cfg.NRB // 2
    assert RBL % 4 == 0 and cfg.NRB == 2 * cfg.rb[(1, "gu1")]
    blobs = [di(f"blob{l}", [RBL * 128, 4096], BF16) for l in range(2)]
    shard_bf = di("shard_bf", [K, 128, 4096], BF16)
    xT_A = di("xT_A", [D, T], BF16)
    xT_B = di("xT_B", [D, T], BF16)
    xg_src = di("xg_src", [D, T], BF16)
    xg_dst = di("xg_dst", [cfg.KC, 512, T], BF16)
    res = [di(f"res{i}", [T, D], F32) for i in range(3)]
    y_scr = di("y_scr", [T, D], F32)
    qT_scr = di("qT_scr", [4, 128, S], BF16)
    kT_scr = di("kT_scr", [4, 128, S], BF16)
    v_scr = di("v_scr", [S, 512], BF16)
    lf_scr = di("lf_scr", [4, S], F32)
    negc_scr = di("negc_scr", [4, S], F32)
    oT_src = di("oT_src", [4, 4, 128, T], BF16)
    oT_dst = di("oT_dst", [4, 4, 512, T], BF16)
    oT_dst2 = oT_dst.reshape([4 * 4 * 512, T])
    wown = di("wown", [768, 4096], BF16)
    omine = di("omine", [4, 512, T], BF16)
    with ExitStack() as st:
        kb = KB(nc, st)
        P = Prog(nc, kb, cfg)
        P.wown, P.omine = wown, omine
        st.enter_context(nc.Block())
        pid_li = nc.gpsimd.partition_id() % 4
        idf = P.sb(st, "idf", [128, 128], F32); IDF = kb.buf("IDF")
        idb = P.sb(st, "idb", [128, 128], BF16); IDB = kb.buf("IDB")
        kb.dma("sp", idf[:], ident[:, :], IDF, writes=[IDF])
        kb.op("dve", lambda e: e.tensor_copy(out=idb[:], in_=idf[:]), reads=[IDF], writes=[IDB])
        phase0_weights(P, wshard, shard_bf, blobs)
        with ExitStack() as st1:
            ln_sweep(P, st1, None, x_tm, None, None, None, xT_A, 0, T // 128, idb, IDB, plain=True)
            P.bar()
        r_in = x_tm
        for l in range(2):
            fox = (l == 1)
            rb = lambda n: cfg.rb[(l, n)] - l * RBL
            blob = blobs[l]
            P.rb_off = l * RBL
            grow = lambda i: (gb[(l * 3 + i) * 2:(l * 3 + i) * 2 + 1, :], gb[(l * 3 + i) * 2 + 1:(l * 3 + i) * 2 + 2, :])
            g_, b_ = grow(0)
            ffn_phase(P, blob, rb("gu1"), rb("d1"), g_, b_, xT_A, r_in, res[0], xg_src, y_scr, idb, IDB)
            gather_chunks(P, [xg_src[c * 128:(c + 1) * 128, :] for c in range(cfg.KC)], [xg_dst[c] for c in range(cfg.KC)])
            qkv_phase(P, blob, rb("qkv"), xg_dst, qT_scr, kT_scr, v_scr, fox, wf_own, bf_own, lf_scr, pid_li)
            if fox:
                fox_cumsum(P, lf_scr, negc_scr, mt)
            attention_phase(P, fox, qT_scr, kT_scr, v_scr, oT_src, idb, IDB, idf, IDF, tri, negc_scr, relm, lqk, subg,
                            0.8 - 0.6 * math.exp(-0.3 * l))
            gather_chunks(P, [oT_src[fc, tc] for tc in range(4) for fc in range(4)],
                          [oT_dst[tc, fc] for tc in range(4) for fc in range(4)])
            wo_phase(P, blob, rb("wo"), oT_dst2, y_scr, pid_li)
            g_, b_ = grow(1)
            with ExitStack() as st2:
                ln_sweep(P, st2, y_scr, res[0], g_, b_, res[1], xT_B, 0, T // 128, idb, IDB)
                P.bar()
            g_, b_ = grow(2)
            last = (l == 1)
            ffn_phase(P, blob, rb("gu2"), rb("d2"), g_, b_, xT_B, res[1], out if last else res[2], None if last else xT_A,
                      y_scr, idb, IDB)
            r_in = res[2]
        kb.wait_all("pool")
        kb.wait_all("sp")
        build_program.stats = dict(sems=kb.nsem, waits=kb.nwait, counts=dict(kb.ecnt))
    return nc


def _t5_bucket_np(n):
    n = np.maximum(n, 0)
    nf = np.maximum(n, 1).astype(np.float32)
    large = 16 + (np.log(nf / np.float32(16)) / np.float32(math.log(128 / 16)) * np.float32(16)).astype(np.int32)
    large = np.minimum(large, 31)
    return np.where(n < 16, n, large)


def _tile_gu(wg, wu, KC, FC):
    out = np.empty((FC, 128, 2, KC, 128), np.float32)
    for m, w in enumerate((wg, wu)):
        out[:, :, m] = w.reshape(KC, 128, FC, 128).transpose(2, 1, 0, 3)
    return out.reshape(FC, 128, 4096)


def _tile_d(wd, FC):
    t = wd.reshape(FC // 4, 4, 128, 4, 512).transpose(3, 0, 2, 1, 4).reshape(FC, 128, 2048)
    return np.ascontiguousarray(t.reshape(FC // 2, 2, 128, 2048).transpose(0, 2, 1, 3)).reshape(FC // 2, 128, 4096)


def _tile_rows(w, KC):
    N = w.shape[1]
    t = w.reshape(KC, 128, N).transpose(1, 0, 2).reshape(128, KC * N)
    return np.ascontiguousarray(t.reshape(128, KC * N // 4096, 4096).transpose(1, 0, 2))


def pack_inputs(cfg, inp):
    D, S, T, KC, FC = cfg.D, cfg.S, cfg.T, cfg.KC, cfg.FC
    f32 = lambda a: np.asarray(a, dtype=np.float32)
    rbs = np.zeros((cfg.NRB, 128, 4096), np.float32)
    for l in range(2):
        rbs[cfg.rb[(l, "gu1")]:cfg.rb[(l, "gu1")] + FC] = _tile_gu(f32(inp["ffn1_wg"][l]), f32(inp["ffn1_wu"][l]), KC, FC)
        rbs[cfg.rb[(l, "d1")]:cfg.rb[(l, "d1")] + FC // 2] = _tile_d(f32(inp["ffn1_wd"][l]), FC)
        rbs[cfg.rb[(l, "gu2")]:cfg.rb[(l, "gu2")] + FC] = _tile_gu(f32(inp["ffn2_wg"][l]), f32(inp["ffn2_wu"][l]), KC, FC)
        rbs[cfg.rb[(l, "d2")]:cfg.rb[(l, "d2")] + FC // 2] = _tile_d(f32(inp["ffn2_wd"][l]), FC)
        wqkv = f32(inp["diff_wqkv"][0] if l == 0 else inp["fox_wqkv"][0])
        wo = f32(inp["diff_wo"][0] if l == 0 else inp["fox_wo"][0])
        for r in range(4):
            own = np.concatenate([wqkv[:, r * 512:(r + 1) * 512], wqkv[:, D + r * 512:D + (r + 1) * 512],
                                  wqkv[:, 2 * D + r * 512:2 * D + (r + 1) * 512]], axis=1)
            b0 = cfg.rb[(l, "qkv")] + r * 6
            rbs[b0:b0 + 6] = _tile_rows(own, KC)
        rbs[cfg.rb[(l, "wo")]:cfg.rb[(l, "wo")] + 8] = _tile_rows(wo, KC)
    shards = [np.ascontiguousarray(rbs[r::4]) for r in range(4)]
    gb = np.empty((12, D), np.float32)
    for l in range(2):
        for i in range(3):
            gb[(l * 3 + i) * 2] = inp["ln_g"][l, i]
            gb[(l * 3 + i) * 2 + 1] = inp["ln_b"][l, i]
    kp = np.arange(128)[:, None]; xx = np.arange(1024)[None, :]
    n = xx - kp - 384
    bucket = _t5_bucket_np(n)
    rel = f32(inp["rel_table"])
    ident = np.eye(128, dtype=np.float32)
    tri = (np.arange(128)[None, :] >= np.arange(128)[:, None]).astype(np.float32)
    pp = np.arange(128)
    mt = ((pp[:, None] // 32 == pp[None, :] // 32) & (pp[:, None] < pp[None, :])).astype(np.float32)
    lqk = np.stack([f32(inp["diff_lq1"][0]), f32(inp["diff_lk1"][0]), f32(inp["diff_lq2"][0]), f32(inp["diff_lk2"][0])])
    subg = f32(inp["diff_subln_g"][0]).reshape(1, 256)
    wf = f32(inp["fox_wf"][0]); bfv = f32(inp["fox_bf"][0])
    x = inp["x"]
    in_maps = []
    for core in range(8):
        b, li = divmod(core, 4)
        relm = np.empty((2, 128, 1024), np.float32)
        for hl in range(2):
            relm[hl] = np.where(n >= 0, rel[bucket, 2 * li + hl], np.float32(NEG))
        wf_own = np.ascontiguousarray(wf[:, 4 * li:4 * li + 4].reshape(KC, 128, 4).transpose(1, 0, 2)).reshape(128, KC * 4)
        in_maps.append({
            "x_tm": np.ascontiguousarray(f32(x[b, li * T:(li + 1) * T, :])),
            "wshard": shards[li], "gb": gb, "relm": relm, "lqk": lqk, "subg": subg,
            "wf_own": wf_own, "bf_own": np.ascontiguousarray(bfv[4 * li:4 * li + 4].reshape(4, 1)),
            "ident": ident, "tri": tri, "mt": mt,
        })
    return in_maps


def run(cfg, inp, trace=False):
    nc = build_program(cfg)
    in_maps = pack_inputs(cfg, inp)
    res = run_bass_kernel_spmd(nc, in_maps, core_ids=list(range(8)))
    out = np.empty((2, cfg.S, cfg.D), np.float32)
    for core in range(8):
        b, li = divmod(core, 4)
        out[b, li * cfg.T:(li + 1) * cfg.T] = res.results[core]["out"]
    return out


def kernel(**inputs):
    cfg = Cfg()
    return run(cfg, inputs)
```
